# Optimizing a Trainium2 kernel written in Bass

```python
import math
import jax, jax.numpy as jnp
from jax import lax
import numpy as np

D_MODEL = 1024
BATCH = 1
SEQ = 16384
DEPTH = 1
DEC_BATCH = 8
DEC_SEQ = 4096
PAST_LEN = 128

GLA_HEADS = 4
GLA_DK = D_MODEL // 2
GLA_DV = D_MODEL
GLA_HK = GLA_DK // GLA_HEADS
GLA_HV = GLA_DV // GLA_HEADS
GLA_RANK = 16
GLA_LOGIT_NORM = 16.0
GLA_CHUNK = 64
DIFF_HEAD_DIM = 64
DIFF_HEADS = D_MODEL // (2 * DIFF_HEAD_DIM)
DIFF_QK = DIFF_HEADS * 2 * DIFF_HEAD_DIM
DIFF_DV = DIFF_HEADS * 2 * DIFF_HEAD_DIM
ROPE_THETA = 10000.0
Q_BLOCK = 128
EPS = 1e-6

SPLIT_SIZES = (GLA_DK, GLA_DK, GLA_DV, GLA_DV, 2 * GLA_RANK,
               DIFF_QK, DIFF_QK, DIFF_DV, DIFF_DV, D_MODEL, D_MODEL)
IN_COLS = sum(SPLIT_SIZES)

kernel_name = "hybrid_gla_diffattn_encoder"


def rms_norm(x, gain):
    xf = x.astype(jnp.float32)
    y = xf * lax.rsqrt(jnp.mean(xf * xf, axis=-1, keepdims=True) + EPS)
    return y.astype(x.dtype) * gain


def rope(x):
    L, d = x.shape[1], x.shape[-1]
    half = d // 2
    inv = 1.0 / (ROPE_THETA ** (jnp.arange(half, dtype=jnp.float32) / half))
    ang = jnp.arange(L, dtype=jnp.float32)[:, None] * inv[None, :]
    cos = jnp.cos(ang)[None, :, None, None, :]
    sin = jnp.sin(ang)[None, :, None, None, :]
    xf = x.astype(jnp.float32)
    x1, x2 = xf[..., :half], xf[..., half:]
    out = jnp.concatenate([x1 * cos - x2 * sin, x1 * sin + x2 * cos], axis=-1)
    return out.astype(x.dtype)


def gla_chunk_step(state, inp):
    q, k, v, g = inp
    C = q.shape[-2]
    b = jnp.cumsum(g, axis=-2)
    o_inter = jnp.einsum('zbhcd,zbhde->zbhce', q * jnp.exp(b), state)
    causal = jnp.tril(jnp.ones((C, C), dtype=bool))
    rel = b[..., :, None, :] - b[..., None, :, :]
    decay = jnp.exp(jnp.where(causal[:, :, None], rel, -jnp.inf))
    scores = jnp.einsum('zbhid,zbhjd,zbhijd->zbhij', q, k, decay)
    o = o_inter + jnp.einsum('zbhij,zbhje->zbhie', scores, v)
    b_last = b[..., -1:, :]
    state = (jnp.exp(b_last[..., 0, :])[..., None] * state
             + jnp.einsum('zbhcd,zbhce->zbhde', k * jnp.exp(b_last - b), v))
    return state, o


def bidirectional_gla(q, k, v, g_fwd, g_bwd):
    B, L, H, dk = q.shape
    dv = v.shape[-1]
    n = L // GLA_CHUNK
    flip = lambda t: jnp.flip(t, axis=1)

    def to_chunks(fwd, bwd):
        t = jnp.stack([fwd, bwd], axis=0).astype(jnp.float32)
        t = t.reshape(2, B, n, GLA_CHUNK, H, t.shape[-1])
        return t.transpose(2, 0, 1, 4, 3, 5)

    xs = (to_chunks(q, flip(q)), to_chunks(k, flip(k)),
          to_chunks(v, flip(v)), to_chunks(g_fwd, flip(g_bwd)))
    state0 = jnp.zeros((2, B, H, dk, dv), jnp.float32)
    _, o = lax.scan(gla_chunk_step, state0, xs)
    o = o.transpose(1, 2, 0, 4, 3, 5).reshape(2, B, L, H, dv)
    return (o[0] + flip(o[1])).astype(v.dtype)


def diff_attention(q, k, v, lam):
    B, L, H, _, d = q.shape
    nb = L // Q_BLOCK
    scale = d ** -0.5
    qb = q.reshape(B, nb, Q_BLOCK, H, 2, d).transpose(1, 0, 2, 3, 4, 5)

    def block(qblk):
        s = jnp.einsum('bqhzd,bkhzd->bhzqk', qblk, k,
                       preferred_element_type=jnp.float32) * scale
        p = jax.nn.softmax(s, axis=-1)
        p = p[:, :, 0] - lam * p[:, :, 1]
        return jnp.einsum('bhqk,bkhe->bqhe', p.astype(v.dtype), v)

    o = lax.map(block, qb)
    return o.transpose(1, 0, 2, 3, 4).reshape(B, L, H, 2 * d)


def encoder_layer(x, c, lam_init, w_ada, b_ada, norm_gain, w_in, w_alpha, b_alpha,
                  gla_norm_gain, lambda_q, lambda_k, diff_norm_gain,
                  w_bo_gla, w_bo_diff, w_out):
    B, L, _ = x.shape
    mod = jax.nn.silu(c) @ w_ada + b_ada
    shift, scale, gate = jnp.split(mod, 3, axis=-1)
    h = rms_norm(x, norm_gain) * (1.0 + scale[:, None, :]) + shift[:, None, :]

    proj = h @ w_in
    points = np.cumsum(SPLIT_SIZES)[:-1].tolist()
    (a_q, a_k, a_v, a_z, a_low, d_q, d_k, d_v, d_z, m_gla, m_diff) = jnp.split(proj, points, axis=-1)

    q = (a_q * (GLA_HK ** -0.5)).reshape(B, L, GLA_HEADS, GLA_HK)
    k = a_k.reshape(B, L, GLA_HEADS, GLA_HK)
    v = a_v.reshape(B, L, GLA_HEADS, GLA_HV)
    low = a_low.reshape(B, L, 2, GLA_RANK)
    logits = jnp.einsum('blzr,zrk->blzk', low, w_alpha) + b_alpha
    log_alpha = jax.nn.log_sigmoid(logits.astype(jnp.float32)) / GLA_LOGIT_NORM
    g_fwd = log_alpha[:, :, 0].reshape(B, L, GLA_HEADS, GLA_HK)
    g_bwd = log_alpha[:, :, 1].reshape(B, L, GLA_HEADS, GLA_HK)
    o_gla = bidirectional_gla(q, k, v, g_fwd, g_bwd)
    o_gla = rms_norm(o_gla, gla_norm_gain).reshape(B, L, GLA_DV) * jax.nn.silu(a_z)
    y_gla = o_gla @ w_bo_gla

    lq = lambda_q.astype(jnp.float32)
    lk = lambda_k.astype(jnp.float32)
    lam = jnp.exp(jnp.sum(lq[0] * lk[0])) - jnp.exp(jnp.sum(lq[1] * lk[1])) + lam_init
    dq = rope(d_q.reshape(B, L, DIFF_HEADS, 2, DIFF_HEAD_DIM))
    dk = rope(d_k.reshape(B, L, DIFF_HEADS, 2, DIFF_HEAD_DIM))
    dv = d_v.reshape(B, L, DIFF_HEADS, 2 * DIFF_HEAD_DIM)
    o_diff = diff_attention(dq, dk, dv, lam)
    o_diff = rms_norm(o_diff, diff_norm_gain) * (1.0 - lam_init)
    o_diff = o_diff.reshape(B, L, DIFF_DV) * jax.nn.silu(d_z)
    y_diff = o_diff @ w_bo_diff

    merged = jax.nn.sigmoid(m_gla) * y_gla + jax.nn.sigmoid(m_diff) * y_diff
    return x + gate[:, None, :] * (merged @ w_out)


def setup_inputs(seed: int = 0) -> dict:
    key = jax.random.key(seed)
    ks = jax.random.split(key, 20)
    f32 = jnp.float32
    nrm = lambda k, shape, s: jax.random.normal(k, shape, f32) * s
    return {
        "x_prompt": nrm(ks[0], (BATCH, SEQ, D_MODEL), 1.0),
        "x_sample": nrm(ks[1], (DEC_BATCH, DEC_SEQ, D_MODEL), 1.0),
        "c_prompt": nrm(ks[2], (BATCH, D_MODEL), 1.0),
        "c_sample": nrm(ks[3], (DEC_BATCH, D_MODEL), 1.0),
        "w_ada": nrm(ks[4], (DEPTH, D_MODEL, 3 * D_MODEL), D_MODEL ** -0.5),
        "b_ada": nrm(ks[5], (DEPTH, 3 * D_MODEL), 0.01),
        "norm_gain": 1.0 + nrm(ks[6], (DEPTH, D_MODEL), 0.01),
        "w_in": nrm(ks[7], (DEPTH, D_MODEL, IN_COLS), D_MODEL ** -0.5),
        "w_alpha": nrm(ks[8], (DEPTH, 2, GLA_RANK, GLA_DK), GLA_RANK ** -0.5),
        "b_alpha": nrm(ks[9], (DEPTH, 2, GLA_DK), 0.1),
        "gla_norm_gain": 1.0 + nrm(ks[10], (DEPTH, GLA_HV), 0.01),
        "lambda_q": nrm(ks[11], (DEPTH, 2, DIFF_HEAD_DIM), 0.1),
        "lambda_k": nrm(ks[12], (DEPTH, 2, DIFF_HEAD_DIM), 0.1),
        "diff_norm_gain": 1.0 + nrm(ks[13], (DEPTH, 2 * DIFF_HEAD_DIM), 0.01),
        "w_bo_gla": nrm(ks[14], (DEPTH, GLA_DV, D_MODEL), GLA_DV ** -0.5),
        "w_bo_diff": nrm(ks[15], (DEPTH, DIFF_DV, D_MODEL), DIFF_DV ** -0.5),
        "w_out": nrm(ks[16], (DEPTH, D_MODEL, D_MODEL), D_MODEL ** -0.5),
        "final_gain": 1.0 + nrm(ks[17], (D_MODEL,), 0.01),
    }


def reference(x_prompt, x_sample, c_prompt, c_sample, w_ada, b_ada, norm_gain, w_in,
              w_alpha, b_alpha, gla_norm_gain, lambda_q, lambda_k, diff_norm_gain,
              w_bo_gla, w_bo_diff, w_out, final_gain):
    def trunk(x, c):
        for layer in range(DEPTH):
            lam_init = 0.8 - 0.6 * math.exp(-0.3 * layer)
            x = encoder_layer(x, c, lam_init, w_ada[layer], b_ada[layer], norm_gain[layer],
                              w_in[layer], w_alpha[layer], b_alpha[layer],
                              gla_norm_gain[layer], lambda_q[layer], lambda_k[layer],
                              diff_norm_gain[layer], w_bo_gla[layer], w_bo_diff[layer],
                              w_out[layer])
        return rms_norm(x, final_gain)

    y_prompt = trunk(x_prompt, c_prompt)
    y_sample = trunk(x_sample, c_sample)
    return (y_prompt, y_sample)
```

```python
import math
from contextlib import ExitStack

import numpy as np
import concourse.bass as bass
import concourse.mybir as mybir
from concourse.bass_utils import run_bass_kernel_spmd

F32 = mybir.dt.float32
BF16 = mybir.dt.bfloat16
AF = mybir.ActivationFunctionType
ALU = mybir.AluOpType
AX = mybir.AxisListType

D = 1024
NC_ = 8
TS = 4096
TO = 2048
TP = 16384
TL = TS + TO
NCH = TL // 128
WCOLS = 7200
EPS = 1e-6
LAM_INIT = 0.8 - 0.6 * math.exp(0.0)
QS = 128.0 ** -0.5

C_AQ, C_AK, C_AV, C_AZ, C_LOW, C_DQ, C_DK, C_DV, C_DZ, C_MG, C_MD = (
    0, 512, 1024, 2048, 3072, 3104, 4128, 5152, 6176, 7200, 8224)


class Obj:
    __slots__ = ("lw", "lr")

    def __init__(self):
        self.lw = {}
        self.lr = {}


class Tl:
    def __init__(self, t):
        self.t = t
        self.o = Obj()


class Sync:
    ENG = ["pe", "act", "dve", "pool", "sp"]

    def __init__(self, nc, stack):
        self.nc = nc
        self.stack = stack
        self.e = {"pe": nc.tensor, "act": nc.scalar, "dve": nc.vector, "pool": nc.gpsimd, "sp": nc.sync}
        self.semobj = {}
        self.cnt = {}
        self.known = {k: {} for k in self.ENG}
        for k in self.ENG:
            self.newsem(k)

    def newsem(self, key):
        self.semobj[key] = self.stack.enter_context(self.nc.semaphore("s_" + key))
        self.cnt[key] = 0

    def _waits(self, eng, reads, writes, extra=None):
        w = {}
        for t in reads:
            for k, v in t.o.lw.items():
                if w.get(k, 0) < v:
                    w[k] = v
        for t in writes:
            for d in (t.o.lw, t.o.lr):
                for k, v in d.items():
                    if w.get(k, 0) < v:
                        w[k] = v
        if extra:
            for k, v in extra.items():
                if w.get(k, 0) < v:
                    w[k] = v
        kn = self.known[eng]
        for k, v in w.items():
            if kn.get(k, 0) < v:
                self.e[eng].wait_ge(self.semobj[k], v)
                kn[k] = v

    def op(self, eng, fn, R=(), W=()):
        self._waits(eng, R, W)
        inst = fn(self.e[eng])
        self.cnt[eng] += 1
        inst.then_inc(self.semobj[eng], 1)
        v = self.cnt[eng]
        for t in R:
            t.o.lr[eng] = v
        for t in W:
            t.o.lw[eng] = v

    def dma(self, key, out, in_, R=(), W=(), q="sp"):
        if key not in self.semobj:
            self.newsem(key)
        self._waits(q, R, W, extra={key: self.cnt[key]})
        inst = self.e[q].dma_start(out=out, in_=in_)
        self.cnt[key] += 16
        inst.then_inc(self.semobj[key], 16)
        v = self.cnt[key]
        for t in R:
            t.o.lr[key] = v
        for t in W:
            t.o.lw[key] = v

    def barrier(self, engs=None):
        allv = dict(self.cnt)
        for eng in (engs or self.ENG):
            kn = self.known[eng]
            for k, v in allv.items():
                if v > 0 and kn.get(k, 0) < v:
                    self.e[eng].wait_ge(self.semobj[k], v)
                    kn[k] = v


class Banks:
    def __init__(self, nc, stack, name, nbanks, dt):
        per = 512 if dt == F32 else 1024
        self.per = per
        self.t = stack.enter_context(nc.psum_tensor(name, [128, nbanks * per], dt))
        self.tl = [Tl(self.t) for _ in range(nbanks)]
        self.n = nbanks
        self.i = 0

    def ap(self, b):
        return self.t[:, b * self.per:(b + 1) * self.per]

    def next(self):
        b = self.i
        self.i = (self.i + 1) % self.n
        return b, self.ap(b), self.tl[b]


def build():
    nc = bass.Bass("TRN2", target_bir_lowering=False)

    def din(name, shape, dt=F32):
        return nc.dram_tensor(name, list(shape), dt, kind="ExternalInput").ap()

    def dscr(name, shape, dt):
        return Tl(nc.dram_tensor(name, list(shape), dt, kind="Internal").ap())

    xs = din("xs", [TS, D]); xo = din("xo", [TO, D]); xr = din("xr", [TP - TO, D])
    cT = din("cT", [128, 8, 2])
    w_ada = din("w_ada", [D, 3 * D]); b_adaT = din("b_adaT", [128, 24]); b_gate = din("b_gate", [1, D])
    ngT = din("ngT", [128, 8])
    w_in = din("w_in", [D, 9248])
    walpha = din("walpha", [33, 1024])
    gla_gain = din("gla_gain", [1, 1024])
    lqk = din("lqk", [1, 256])
    diff_gain = din("diff_gain", [1, 128])
    w_bog = din("w_bog", [D, D]); w_bod = din("w_bod", [D, D]); w_outd = din("w_out", [D, D])
    fgain = din("fgain", [1, D])
    ident_d = din("ident", [128, 128])
    tri_d = din("tri", [4, 128, 128])
    msk_d = din("msk", [2, 128, 128])
    negs_d = din("negs", [128, 1])
    rope_t = din("rope_t", [TP, 64])
    rope_o = din("rope_o", [TO, 64])
    rope_r = din("rope_r", [TP - TO, 64])
    segm = din("segm", [1, 512])
    y_s = nc.dram_tensor("y_s", [TS, D], F32, kind="ExternalOutput").ap()
    y_o = nc.dram_tensor("y_o", [TO, D], F32, kind="ExternalOutput").ap()

    QT = dscr("QT", [8, 128, TL], BF16)
    KTs = dscr("KTs", [8, 128, TS], BF16); KTp = dscr("KTp", [8, 128, TP], BF16)
    Vs = dscr("Vs", [8, 128, TS // 128, 129], BF16); Vp = dscr("Vp", [8, 128, TP // 128, 129], BF16)
    HT = dscr("HT", [NCH, 128, 1024], BF16)
    ZG = dscr("ZG", [TL, D], BF16); ZD = dscr("ZD", [TL, D], BF16)
    OG = dscr("OG", [TL, D], BF16); OD = dscr("OD", [TL, D], BF16)
    OPART = dscr("OPART", [NCH, 128, 1024], F32)
    QBT = dscr("QBT", [NCH, 128, 512], BF16)
    UB = dscr("UB", [NCH, 128, 1024], F32)
    DECB = dscr("DECB", [NCH, 128, 4], F32)
    GATE = dscr("GATE", [128, 2048], F32)

    with ExitStack() as st0:
        S = Sync(nc, st0)

        def sb(stack, name, shape, dt):
            return Tl(stack.enter_context(nc.sbuf_tensor("sb_" + name, list(shape), dt)))

        ident = sb(st0, "ident", [128, 128], F32)
        identb = sb(st0, "identb", [128, 128], BF16)
        Aco = sb(st0, "Aco", [128, 2, 8], F32)
        Bco = sb(st0, "Bco", [128, 2, 8], F32)
        nlam = sb(st0, "nlam", [128, 1], F32)
        gd = sb(st0, "gd", [128, 128], F32)
        Sb_in = sb(st0, "Sb_in", [128, 1024], F32)
        Sf = sb(st0, "Sf", [128, 1024], F32)
        S.dma("c_id", ident.t[:], ident_d, W=[ident])
        S.op("act", lambda e: e.copy(identb.t[:], ident.t[:]), R=[ident], W=[identb])

        with ExitStack() as st:
            ps = Banks(nc, st, "ps0", 6, F32)
            c_sb = sb(st, "c_sb", [128, 8, 2], F32)
            gate_bc = sb(st, "gate_bc", [128, 2, 1024], F32)
            sc = sb(st, "sc", [128, 8, 2], F32)
            sc_bc = sb(st, "sc_bc", [128, 2, 8, 128], F32)
            wa = [sb(st, "wa%d" % i, [128, 3072], F32) for i in range(2)]
            badT = sb(st, "badT", [128, 24], F32)
            ngs = sb(st, "ngs", [128, 8], F32)
            bg = sb(st, "bg", [128, 1024], F32)
            lq = sb(st, "lq", [128, 256], F32)
            pr = sb(st, "pr", [128, 128], F32)
            e2 = sb(st, "e2", [128, 2], F32)
            tmp8 = sb(st, "tmp8", [128, 8], F32)
            dg = sb(st, "dg", [128, 128], F32)
            S.dma("p0a", c_sb.t[:], cT, W=[c_sb])
            S.dma("p0b", badT.t[:], b_adaT, W=[badT])
            S.dma("p0c", ngs.t[:], ngT, W=[ngs])
            S.dma("p0d", bg.t[:], b_gate.partition_broadcast(128), W=[bg])
            S.dma("p0e", lq.t[:], lqk.partition_broadcast(128), W=[lq])
            S.dma("p0f", dg.t[:], diff_gain.partition_broadcast(128), W=[dg])
            S.op("act", lambda e: e.activation(sc.t[:], c_sb.t[:], AF.Silu), R=[c_sb], W=[sc])
            for s in range(2):
                S.op("dve", lambda e: e.tensor_copy(
                    sc_bc.t[:, s], sc.t[:, :, s:s + 1].to_broadcast([128, 8, 128])), R=[sc], W=[sc_bc])
            _, modp, modt = ps.next()
            gps = [ps.next() for _ in range(4)]
            for kc in range(8):
                w = wa[kc % 2]
                S.dma("wa%d" % (kc % 2), w.t[:], w_ada[kc * 128:(kc + 1) * 128, :], W=[w])

                def mm(e):
                    last = None
                    for j in range(16):
                        last = e.matmul(modp[:, j * 2:(j + 1) * 2], w.t[:, j * 128:(j + 1) * 128],
                                        sc.t[:, kc, :], start=(kc == 0 and j == 0), stop=(kc == 7),
                                        skip_group_check=True)
                    return last
                S.op("pe", mm, R=[w, sc], W=[modt])
                for s in range(2):
                    for hf in range(2):
                        _, gp, gt = gps[s * 2 + hf]
                        S.op("pe", lambda e: e.matmul(
                            gp, sc_bc.t[:, s, kc, :], w.t[:, 2048 + hf * 512:2048 + (hf + 1) * 512],
                            start=(kc == 0), stop=(kc == 7)), R=[w, sc_bc], W=[gt])
            modv = modp[:, 0:32].rearrange("p (j s) -> p j s", s=2)
            for s in range(2):
                S.op("dve", lambda e: e.tensor_tensor(Bco.t[:, s, :], modv[:, 0:8, s], badT.t[:, 0:8], op=ALU.add),
                     R=[modt, badT], W=[Bco])
                S.op("dve", lambda e: e.tensor_tensor(tmp8.t[:], modv[:, 8:16, s], badT.t[:, 8:16], op=ALU.add),
                     R=[modt, badT], W=[tmp8])
                S.op("dve", lambda e: e.scalar_tensor_tensor(
                    Aco.t[:, s, :], tmp8.t[:], 1.0, ngs.t[:], op0=ALU.add, op1=ALU.mult),
                    R=[tmp8, ngs], W=[Aco])
                for hf in range(2):
                    _, gp, gt = gps[s * 2 + hf]
                    S.op("dve", lambda e: e.tensor_tensor(
                        gate_bc.t[:, s, hf * 512:(hf + 1) * 512], gp, bg.t[:, hf * 512:(hf + 1) * 512], op=ALU.add),
                        R=[gt, bg], W=[gate_bc])
            S.dma("st_gate", GATE.t, gate_bc.t[:].rearrange("p s c -> p (s c)"), R=[gate_bc], W=[GATE])
            S.op("dve", lambda e: e.tensor_tensor(pr.t[:], lq.t[:, 0:128], lq.t[:, 128:256], op=ALU.mult), R=[lq], W=[pr])
            S.op("dve", lambda e: e.tensor_reduce(
                e2.t[:], pr.t[:].rearrange("p (z d) -> p z d", z=2), axis=AX.X, op=ALU.add), R=[pr], W=[e2])
            S.op("act", lambda e: e.activation(e2.t[:], e2.t[:], AF.Exp), R=[e2], W=[e2])
            S.op("dve", lambda e: e.tensor_tensor(nlam.t[:], e2.t[:, 1:2], e2.t[:, 0:1], op=ALU.subtract), R=[e2], W=[nlam])
            S.op("dve", lambda e: e.tensor_scalar(nlam.t[:], nlam.t[:], -LAM_INIT, None, op0=ALU.add), R=[nlam], W=[nlam])
            S.op("dve", lambda e: e.tensor_scalar(gd.t[:], dg.t[:], 1.0 - LAM_INIT, None, op0=ALU.mult), R=[dg], W=[gd])
            S.barrier()

        with ExitStack() as st:
            W = sb(st, "Wres", [128, 8, WCOLS], BF16)
            with ExitStack() as stw:
                stg = [sb(stw, "wstg%d" % i, [128, 2400], F32) for i in range(2)]
                i = 0
                for kc in range(8):
                    for c0 in range(0, WCOLS, 2400):
                        g = stg[i % 2]
                        S.dma("wstg%d" % (i % 2), g.t[:], w_in[kc * 128:(kc + 1) * 128, c0:c0 + 2400], W=[g])
                        eng = ["dve", "pool", "act"][i % 3]
                        if eng == "act":
                            S.op("act", lambda e: e.copy(W.t[:, kc, c0:c0 + 2400], g.t[:]), R=[g], W=[W])
                        else:
                            S.op(eng, lambda e: e.tensor_copy(W.t[:, kc, c0:c0 + 2400], g.t[:]), R=[g], W=[W])
                        i += 1
                S.barrier()
            psf = Banks(nc, st, "psf", 6, F32)
            psb = Banks(nc, st, "psb", 2, BF16)
            tri = sb(st, "tri", [128, 4, 128], F32)
            msk = sb(st, "msk", [128, 2, 128], F32)
            negs = sb(st, "negs", [128, 1], F32)
            wal = sb(st, "wal", [33, 1024], F32)
            seg = sb(st, "seg", [128, 512], F32)
            lowT = sb(st, "lowT", [33, 128], F32)
            S.dma("c_tri", tri.t[:], tri_d.rearrange("k p i -> p k i"), W=[tri])
            S.dma("c_msk", msk.t[:], msk_d.rearrange("k p i -> p k i"), W=[msk])
            S.dma("c_neg", negs.t[:], negs_d, W=[negs])
            S.dma("c_wal", wal.t[:], walpha, W=[wal])
            S.dma("c_seg", seg.t[:], segm.partition_broadcast(128), W=[seg])
            S.op("dve", lambda e: e.memset(lowT.t[32:33, :], 1.0), W=[lowT])
            xt = [sb(st, "xt%d" % i, [128, 1024], F32) for i in range(2)]
            cs = [sb(st, "cs%d" % i, [128, 64], F32) for i in range(2)]
            junk = sb(st, "junk", [128, 1024], BF16)
            st2 = sb(st, "st2", [128, 2], F32)
            xn = sb(st, "xn", [128, 1024], F32)
            hT = [sb(st, "hT%d" % i, [128, 8, 128], BF16) for i in range(2)]
            lsps = [sb(st, "lsp%d" % i, [128, 1024], F32) for i in range(2)]
            Ea = sb(st, "Ea", [128, 512], F32)
            Eb = sb(st, "Eb", [128, 512], F32)
            qfT = sb(st, "qfT", [128, 512], BF16); kfT = sb(st, "kfT", [128, 512], BF16)
            qbT = sb(st, "qbT", [128, 512], BF16); kbT = sb(st, "kbT", [128, 512], BF16)
            khf = sb(st, "khf", [128, 512], BF16); khb = sb(st, "khb", [128, 512], BF16)
            dec = sb(st, "dec", [128, 8], F32)
            vtm = sb(st, "vtm", [128, 1024], BF16)
            m1 = sb(st, "m1", [128, 512], F32); m2 = sb(st, "m2", [128, 512], F32)
            PT = sb(st, "PT", [128, 512], BF16)
            oev = sb(st, "oev", [128, 1024], F32)
            ubev = sb(st, "ubev", [128, 1024], F32)
            Sf_bf = sb(st, "Sf_bf", [128, 1024], BF16)
            Dp = sb(st, "Dp", [128, 4], F32)
            co4 = sb(st, "co4", [128, 8], F32)
            zst = [sb(st, "zst%d" % i, [128, 1024], BF16) for i in range(2)]
            rt = [sb(st, "rt%d" % i, [128, 8, 32], F32) for i in range(4)]
            rq = sb(st, "rq", [128, 1024], BF16)
            qkT = [sb(st, "qkT%d" % i, [128, 8, 128], BF16) for i in range(2)]
            vaug = sb(st, "vaug", [128, 8, 129], BF16)
            S.op("pool", lambda e: e.memset(vaug.t[:], 1.0), W=[vaug])
            S.op("pool", lambda e: e.memset(Sf.t[:], 0.0), W=[Sf])
            S.op("pool", lambda e: e.memset(Sb_in.t[:], 0.0), W=[Sb_in])
            S.op("pool", lambda e: e.memset(Sf_bf.t[:], 0.0), W=[Sf_bf])
            S.op("pool", lambda e: e.memset(Dp.t[:], 1.0), W=[Dp])

            def proj_tm(hslot, c0, n=512):
                b, pap, pt = psf.next()

                def f(e):
                    last = None
                    for kc in range(8):
                        last = e.matmul(pap[:, 0:n], hslot.t[:, kc, :], W.t[:, kc, c0:c0 + n],
                                        start=(kc == 0), stop=(kc == 7))
                    return last
                S.op("pe", f, R=[hslot, W], W=[pt])
                return pap, pt

            def proj_fm(hslot, c0, ngrp, m=128):
                b, pap, pt = psf.next()

                def f(e):
                    last = None
                    for g in range(ngrp):
                        for kc in range(8):
                            last = e.matmul(pap[0:m, g * 128:(g + 1) * 128], W.t[:, kc, c0 + g * m:c0 + (g + 1) * m],
                                            hslot.t[:, kc, :], start=(kc == 0), stop=(kc == 7))
                    return last
                S.op("pe", f, R=[hslot, W], W=[pt])
                return pap, pt

            def rope_store(hslot, c0, cst, dstT, t0, slot):
                for g in range(2):
                    pap, pt = proj_tm(hslot, c0 + g * 512)
                    xv = pap.rearrange("p (b d) -> p b d", d=64)
                    cosb = cst.t[:, 0:32].unsqueeze(1).to_broadcast([128, 8, 32])
                    sinb = cst.t[:, 32:64].unsqueeze(1).to_broadcast([128, 8, 32])
                    S.op("dve", lambda e: e.tensor_tensor(rt[0].t[:], xv[:, :, 0:32], cosb, op=ALU.mult), R=[pt, cst], W=[rt[0]])
                    S.op("dve", lambda e: e.tensor_tensor(rt[1].t[:], xv[:, :, 32:64], sinb, op=ALU.mult), R=[pt, cst], W=[rt[1]])
                    S.op("dve", lambda e: e.tensor_tensor(rt[2].t[:], xv[:, :, 0:32], sinb, op=ALU.mult), R=[pt, cst], W=[rt[2]])
                    S.op("dve", lambda e: e.tensor_tensor(rt[3].t[:], xv[:, :, 32:64], cosb, op=ALU.mult), R=[pt, cst], W=[rt[3]])
                    rv = rq.t[:, g * 512:(g + 1) * 512].rearrange("p (b d) -> p b d", d=64)
                    S.op("pool", lambda e: e.tensor_tensor(rv[:, :, 0:32], rt[0].t[:], rt[1].t[:], op=ALU.subtract),
                         R=[rt[0], rt[1]], W=[rq])
                    S.op("pool", lambda e: e.tensor_tensor(rv[:, :, 32:64], rt[2].t[:], rt[3].t[:], op=ALU.add),
                         R=[rt[2], rt[3]], W=[rq])
                    yield
                b, bap, bt = psb.next()

                def tr(e):
                    last = None
                    for h in range(8):
                        last = e.transpose(bap[:, h * 128:(h + 1) * 128], rq.t[:, h * 128:(h + 1) * 128], identb.t[:])
                    return last
                S.op("pe", tr, R=[rq, identb], W=[bt])
                dst = qkT[slot]
                S.op("act", lambda e: e.copy(dst.t[:].rearrange("p h t -> p (h t)"), bap), R=[bt], W=[dst])
                S.dma("st_qk%d" % slot, dstT.t.rearrange("h p t -> p h t")[:, :, t0:t0 + 128], dst.t[:], R=[dst], W=[dstT])
                yield

            def front_a(td):
                it, xrows, ropesrc, s, mode, lt0, kv_dst, pre = td
                sl = it % 2
                x = xt[sl]; c_ = cs[sl]
                S.dma("ldx%d" % sl, x.t[:], xrows, W=[x])
                S.dma("ldc%d" % sl, c_.t[:], ropesrc, W=[c_])
                S.op("act", lambda e: e.activation(junk.t[:], x.t[:], AF.Square, accum_out=st2.t[:, 0:1]), R=[x], W=[junk, st2])
                S.op("act", lambda e: e.activation(st2.t[:, 1:2], st2.t[:, 0:1], AF.Ln, scale=1.0 / D, bias=EPS), R=[st2], W=[st2])
                S.op("act", lambda e: e.activation(st2.t[:, 1:2], st2.t[:, 1:2], AF.Exp, scale=-0.5), R=[st2], W=[st2])
                S.op("dve", lambda e: e.tensor_scalar(xn.t[:], x.t[:], st2.t[:, 1:2], None, op0=ALU.mult), R=[x, st2], W=[xn])

            def front_b(td):
                it, xrows, ropesrc, s, mode, lt0, kv_dst, pre = td
                sl = it % 2
                h = hT[sl]; lsp = lsps[sl]
                for g in range(2):
                    _, pap, pt = psf.next()

                    def tr(e):
                        last = None
                        for k4 in range(4):
                            kc = g * 4 + k4
                            last = e.transpose(pap[:, k4 * 128:(k4 + 1) * 128], xn.t[:, kc * 128:(kc + 1) * 128], ident.t[:])
                        return last
                    S.op("pe", tr, R=[xn, ident], W=[pt])
                    for k4 in range(4):
                        kc = g * 4 + k4
                        if k4 % 2 == 0:
                            S.op("act", lambda e: e.activation(
                                h.t[:, kc, :], pap[:, k4 * 128:(k4 + 1) * 128], AF.Identity,
                                scale=Aco.t[:, s, kc:kc + 1], bias=Bco.t[:, s, kc:kc + 1]), R=[pt, Aco, Bco], W=[h])
                        else:
                            S.op("dve", lambda e: e.tensor_scalar(
                                h.t[:, kc, :], pap[:, k4 * 128:(k4 + 1) * 128], Aco.t[:, s, kc:kc + 1],
                                Bco.t[:, s, kc:kc + 1], op0=ALU.mult, op1=ALU.add), R=[pt, Aco, Bco], W=[h])
                    yield
                if mode == "full":
                    S.dma("st_ht", HT.t[lt0 // 128], h.t[:].rearrange("p k t -> p (k t)"), R=[h], W=[HT])
                lp, lpt = proj_fm(h, C_LOW, 1, m=32)
                S.op("act", lambda e: e.copy(lowT.t[0:32, :], lp[0:32, 0:128]), R=[lpt], W=[lowT])
                yield
                for z in range(2):
                    _, pap, pt = psf.next()
                    S.op("pe", lambda e: e.matmul(pap, lowT.t[:], wal.t[:, z * 512:(z + 1) * 512], start=True, stop=True),
                         R=[lowT, wal], W=[pt])
                    S.op("act", lambda e: e.activation(lsp.t[:, z * 512:(z + 1) * 512], pap, AF.Exp, scale=-1.0), R=[pt], W=[lsp])
                S.op("act", lambda e: e.activation(lsp.t[:], lsp.t[:], AF.Ln, bias=1.0), R=[lsp], W=[lsp])
                yield

            def chain(td):
                it, xrows, ropesrc, s, mode, lt0, kv_dst, pre = td
                sl = it % 2
                h = hT[sl]; lsp = lsps[sl]
                ch = lt0 // 128 if mode == "full" else None
                if pre is not None:
                    pre()
                if mode == "full":
                    aqp, aqt = proj_fm(h, C_AQ, 4)
                    akp, akt = proj_fm(h, C_AK, 4)
                    for dr in range(2):
                        _, cp, ct = psf.next()

                        def cm(e):
                            last = None
                            for hh in range(4):
                                last = e.matmul(cp[:, hh * 128:(hh + 1) * 128],
                                                lsp.t[:, dr * 512 + hh * 128:dr * 512 + (hh + 1) * 128],
                                                tri.t[:, dr, :], start=True, stop=True)
                            return last
                        S.op("pe", cm, R=[lsp, tri], W=[ct])
                        S.op("act", lambda e: e.activation(Ea.t[:], cp, AF.Exp), R=[ct], W=[Ea])
                        S.op("act", lambda e: e.activation(Eb.t[:], cp, AF.Exp, scale=-1.0), R=[ct], W=[Eb])
                        qd, kd = (qfT, kfT) if dr == 0 else (qbT, kbT)
                        S.op("dve", lambda e: e.tensor_tensor(qd.t[:], aqp, Ea.t[:], op=ALU.mult), R=[aqt, Ea], W=[qd])
                        S.op("dve", lambda e: e.tensor_tensor(kd.t[:], akp, Eb.t[:], op=ALU.mult), R=[akt, Eb], W=[kd])
                        yield
                ktp, ktt = proj_tm(h, C_AK)
                for dr in range(2):
                    _, cp, ct = psf.next()
                    S.op("pe", lambda e: e.matmul(cp, tri.t[:, 2 + dr, :], lsp.t[:, dr * 512:(dr + 1) * 512], start=True, stop=True),
                         R=[lsp, tri], W=[ct])
                    Ex = Ea if dr == 0 else Eb
                    S.op("act", lambda e: e.activation(Ex.t[:], cp, AF.Exp), R=[ct], W=[Ex])
                    kh = khf if dr == 0 else khb
                    S.op("dve", lambda e: e.tensor_tensor(kh.t[:], ktp, Ex.t[:], op=ALU.mult), R=[ktt, Ex], W=[kh])
                    yield
                _, tp_, tt_ = psf.next()

                def tot(e):
                    last = None
                    for j in range(8):
                        last = e.matmul(tp_[:, j:j + 1], lsp.t[:, j * 128:(j + 1) * 128], negs.t[:], start=True, stop=True)
                    return last
                S.op("pe", tot, R=[lsp, negs], W=[tt_])
                S.op("act", lambda e: e.activation(dec.t[:], tp_[:, 0:8], AF.Exp), R=[tt_], W=[dec])
                for g in range(2):
                    pap, pt = proj_tm(h, C_AV + g * 512)
                    S.op("act", lambda e: e.copy(vtm.t[:, g * 512:(g + 1) * 512], pap), R=[pt], W=[vtm])
                yield
                if mode == "full":
                    sp_ = []
                    for dr in range(2):
                        _, pap, pt = psf.next()
                        qd, kd = (qfT, kfT) if dr == 0 else (qbT, kbT)

                        def scm(e):
                            last = None
                            for hh in range(4):
                                last = e.matmul(pap[:, hh * 128:(hh + 1) * 128], kd.t[:, hh * 128:(hh + 1) * 128],
                                                qd.t[:, hh * 128:(hh + 1) * 128], start=True, stop=True)
                            return last
                        S.op("pe", scm, R=[qd, kd], W=[pt])
                        sp_.append((pap, pt))
                    for dr in range(2):
                        pap, pt = sp_[dr]
                        mm_ = m1 if dr == 0 else m2
                        S.op("dve", lambda e: e.tensor_tensor(
                            mm_.t[:].rearrange("p (h i) -> p h i", h=4), pap.rearrange("p (h i) -> p h i", h=4),
                            msk.t[:, dr, :].unsqueeze(1).to_broadcast([128, 4, 128]), op=ALU.mult), R=[pt, msk], W=[mm_])
                    S.op("pool", lambda e: e.tensor_tensor(PT.t[:], m1.t[:], m2.t[:], op=ALU.add), R=[m1, m2], W=[PT])
                    yield
                    for g in range(2):
                        _, pap, pt = psf.next()

                        def om(e):
                            last = None
                            for h2 in range(2):
                                hh = g * 2 + h2
                                e.matmul(pap[:, h2 * 256:(h2 + 1) * 256], PT.t[:, hh * 128:(hh + 1) * 128],
                                         vtm.t[:, hh * 256:(hh + 1) * 256], start=True, stop=False)
                                last = e.matmul(pap[:, h2 * 256:(h2 + 1) * 256], qfT.t[:, hh * 128:(hh + 1) * 128],
                                                Sf_bf.t[:, hh * 256:(hh + 1) * 256], start=False, stop=True)
                            return last
                        S.op("pe", om, R=[PT, vtm, qfT, Sf_bf], W=[pt])
                        S.op("act", lambda e: e.activation(oev.t[:, g * 512:(g + 1) * 512], pap, AF.Copy, scale=QS), R=[pt], W=[oev])
                    S.dma("st_op", OPART.t[ch], oev.t[:], R=[oev], W=[OPART])
                    S.dma("st_qb", QBT.t[ch], qbT.t[:], R=[qbT], W=[QBT])
                    S.dma("st_db", DECB.t[ch], dec.t[:, 4:8], R=[dec], W=[DECB])
                    yield
                for dr in range(2):
                    kh = khf if dr == 0 else khb
                    ups = []
                    for g in range(2):
                        _, pap, pt = psf.next()

                        def um(e):
                            last = None
                            for h2 in range(2):
                                hh = g * 2 + h2
                                last = e.matmul(pap[:, h2 * 256:(h2 + 1) * 256], kh.t[:, hh * 128:(hh + 1) * 128],
                                                vtm.t[:, hh * 256:(hh + 1) * 256], start=True, stop=True)
                            return last
                        S.op("pe", um, R=[kh, vtm], W=[pt])
                        ups.append((pap, pt))
                    if mode == "full":
                        if dr == 0:
                            for hh in range(4):
                                pap, pt = ups[hh // 2]
                                S.op("dve", lambda e: e.scalar_tensor_tensor(
                                    Sf.t[:, hh * 256:(hh + 1) * 256], Sf.t[:, hh * 256:(hh + 1) * 256], dec.t[:, hh:hh + 1],
                                    pap[:, (hh % 2) * 256:(hh % 2 + 1) * 256], op0=ALU.mult, op1=ALU.add),
                                    R=[Sf, dec, pt], W=[Sf])
                            S.op("pool", lambda e: e.tensor_copy(Sf_bf.t[:], Sf.t[:]), R=[Sf], W=[Sf_bf])
                        else:
                            for g in range(2):
                                pap, pt = ups[g]
                                S.op("act", lambda e: e.copy(ubev.t[:, g * 512:(g + 1) * 512], pap), R=[pt], W=[ubev])
                            S.dma("st_ub", UB.t[ch], ubev.t[:], R=[ubev], W=[UB])
                    else:
                        n = it
                        if dr == 0:
                            S.op("dve", lambda e: e.scalar_tensor_tensor(
                                co4.t[:, 0:4], dec.t[:, 0:4], seg.t[:, n:n + 1],
                                seg.t[:, 128 + n:129 + n].to_broadcast([128, 4]), op0=ALU.mult, op1=ALU.add),
                                R=[dec, seg], W=[co4])
                            for g in range(2):
                                pap, pt = ups[g]
                                S.op("act", lambda e: e.activation(ubev.t[:, g * 512:(g + 1) * 512], pap, AF.Copy,
                                                                   scale=seg.t[:, n:n + 1]), R=[pt, seg], W=[ubev])
                            for hh in range(4):
                                S.op("dve", lambda e: e.scalar_tensor_tensor(
                                    Sf.t[:, hh * 256:(hh + 1) * 256], Sf.t[:, hh * 256:(hh + 1) * 256], co4.t[:, hh:hh + 1],
                                    ubev.t[:, hh * 256:(hh + 1) * 256], op0=ALU.mult, op1=ALU.add),
                                    R=[Sf, co4, ubev], W=[Sf])
                        else:
                            S.op("dve", lambda e: e.tensor_scalar(co4.t[:, 4:8], Dp.t[:], seg.t[:, 256 + n:257 + n], None, op0=ALU.mult),
                                 R=[Dp, seg], W=[co4])
                            for hh in range(4):
                                pap, pt = ups[hh // 2]
                                S.op("dve", lambda e: e.scalar_tensor_tensor(
                                    Sb_in.t[:, hh * 256:(hh + 1) * 256], pap[:, (hh % 2) * 256:(hh % 2 + 1) * 256],
                                    co4.t[:, 4 + hh:5 + hh], Sb_in.t[:, hh * 256:(hh + 1) * 256], op0=ALU.mult, op1=ALU.add),
                                    R=[Sb_in, co4, pt], W=[Sb_in])
                            S.op("dve", lambda e: e.scalar_tensor_tensor(
                                co4.t[:, 4:8], dec.t[:, 4:8], seg.t[:, 256 + n:257 + n],
                                seg.t[:, 384 + n:385 + n].to_broadcast([128, 4]), op0=ALU.mult, op1=ALU.add),
                                R=[dec, seg], W=[co4])
                            S.op("dve", lambda e: e.tensor_tensor(Dp.t[:], Dp.t[:], co4.t[:, 4:8], op=ALU.mult), R=[Dp, co4], W=[Dp])
                    yield

            def bulk(td):
                it, xrows, ropesrc, s, mode, lt0, kv_dst, pre = td
                sl = it % 2
                h = hT[sl]; c_ = cs[sl]
                if kv_dst is not None:
                    KTd, Vd, t0 = kv_dst
                    yield from rope_store(h, C_DK, c_, KTd, t0, 1)
                    for g in range(2):
                        pap, pt = proj_tm(h, C_DV + g * 512)
                        S.op("act", lambda e: e.copy(vaug.t[:, g * 4:(g + 1) * 4, 0:128],
                                                     pap.rearrange("p (h e) -> p h e", e=128)), R=[pt], W=[vaug])
                    S.dma("st_v", Vd.t[:, :, t0 // 128, :].rearrange("h p e -> p h e"), vaug.t[:], R=[vaug], W=[Vd])
                    yield
                if mode == "full":
                    yield from rope_store(h, C_DQ, c_, QT, lt0, 0)
                    for zi, (c0, dstz) in enumerate(((C_AZ, ZG), (C_DZ, ZD))):
                        zt = zst[zi]
                        for g in range(2):
                            pap, pt = proj_tm(h, c0 + g * 512)
                            S.op("act", lambda e: e.activation(zt.t[:, g * 512:(g + 1) * 512], pap, AF.Silu), R=[pt], W=[zt])
                        S.dma("st_z%d" % zi, dstz.t[lt0:lt0 + 128, :], zt.t[:], R=[zt], W=[dstz])
                        yield

            def interleave(gens):
                gens = list(gens)
                while gens:
                    for g in list(gens):
                        try:
                            next(g)
                        except StopIteration:
                            gens.remove(g)

            def pre_own():
                S.op("pool", lambda e: e.tensor_copy(Sf_bf.t[:], Sf.t[:]), R=[Sf], W=[Sf_bf])

            def pre_sample():
                S.op("pool", lambda e: e.memset(Sf.t[:], 0.0), W=[Sf])
                S.op("pool", lambda e: e.memset(Sf_bf.t[:], 0.0), W=[Sf_bf])

            tds = []
            for n in range((TP - TO) // 128):
                tds.append((n, xr[n * 128:(n + 1) * 128, :], rope_r[n * 128:(n + 1) * 128, :], 1, "kv", None,
                            (KTp, Vp, TO + n * 128), None))
            for n in range(TO // 128):
                tds.append((n, xo[n * 128:(n + 1) * 128, :], rope_o[n * 128:(n + 1) * 128, :], 1, "full", TS + n * 128,
                            (KTp, Vp, n * 128), pre_own if n == 0 else None))
            for n in range(TS // 128):
                tds.append((n, xs[n * 128:(n + 1) * 128, :], rope_t[n * 128:(n + 1) * 128, :], 0, "full", n * 128, (KTs, Vs, n * 128),
                            pre_sample if n == 0 else None))
            front_a(tds[0])
            interleave([front_b(tds[0])])
            for i, td in enumerate(tds):
                gens = [chain(td), bulk(td)]
                if i + 1 < len(tds):
                    front_a(tds[i + 1])
                    gens.append(front_b(tds[i + 1]))
                interleave(gens)
            S.barrier()

        with ExitStack() as st:
            psa = Banks(nc, st, "psa", 8, F32)
            KTt = [sb(st, "KTt%d" % i, [128, TP], BF16) for i in range(2)]
            Vt = [sb(st, "Vt%d" % i, [128, TP // 128, 129], BF16) for i in range(2)]
            QTt = [sb(st, "QTt%d" % i, [128, TS], BF16) for i in range(2)]
            Et = [sb(st, "Et%d" % i, [128, 1024], BF16) for i in range(3)]
            accs = [sb(st, "accs0", [128, 8, 129], F32)] * 2
            rr8 = sb(st, "rr8", [128, 8], F32)
            rl4 = sb(st, "rl4", [128, 4], F32)
            t_o = sb(st, "t_o", [128, 128], F32)
            o_os = [sb(st, "o_o%d" % i, [128, 4, 128], F32) for i in range(2)]
            ss4s = [sb(st, "ss4c%d" % i, [128, 4], F32) for i in range(2)]
            pending = []
            jk2 = sb(st, "jk2", [128, 128], F32)
            odt = [sb(st, "odt%d" % i, [128, 4, 128], BF16) for i in range(2)]
            sc_t = [Tl(None), Tl(None)]
            acc_t = [Tl(None), Tl(None), Tl(None)]
            ggain = sb(st, "ggain", [128, 1024], F32)
            S.dma("c_gg", ggain.t[:], gla_gain.partition_broadcast(128), W=[ggain])
            Sb = sb(st, "Sb", [128, 1024], F32)
            Sb_bf = sb(st, "Sb_bf", [128, 1024], BF16)
            op_ = sb(st, "op_", [128, 1024], F32)
            qb_ = [sb(st, "qb%d" % i, [128, 512], BF16) for i in range(2)]
            ub_ = sb(st, "ub_", [128, 1024], F32)
            db_ = [sb(st, "db%d" % i, [128, 4], F32) for i in range(2)]
            zg_ = [sb(st, "zg0", [128, 1024], BF16)] * 2
            osum = sb(st, "osum", [128, 1024], F32)
            gz = sb(st, "gz", [128, 1024], F32)
            jk = sb(st, "jk", [128, 256], F32)
            ssg = sb(st, "ssg", [128, 4], F32)
            ogt = [sb(st, "ogt0", [128, 1024], BF16)] * 2
            b7 = Tl(None)

            def sweep2():
                it = 0
                for (c_lo, c_hi, init_from) in ((0, TS // 128, None), (TS // 128, NCH, Sb_in)):
                    if init_from is None:
                        S.op("pool", lambda e: e.memset(Sb.t[:], 0.0), W=[Sb])
                    else:
                        S.op("pool", lambda e: e.tensor_copy(Sb.t[:], init_from.t[:]), R=[init_from], W=[Sb])
                    S.op("pool", lambda e: e.tensor_copy(Sb_bf.t[:], Sb.t[:]), R=[Sb], W=[Sb_bf])
                    for ch in range(c_hi - 1, c_lo - 1, -1):
                        sl = it % 2
                        it += 1
                        q_b, d_b, z_g, og_t = qb_[sl], db_[sl], zg_[sl], ogt[sl]
                        S.dma("b_op", op_.t[:], OPART.t[ch], R=[OPART], W=[op_])
                        S.dma("b_qb%d" % sl, q_b.t[:], QBT.t[ch], R=[QBT], W=[q_b])
                        S.dma("b_ub", ub_.t[:], UB.t[ch], R=[UB], W=[ub_])
                        S.dma("b_db%d" % sl, d_b.t[:], DECB.t[ch], R=[DECB], W=[d_b])
                        S.dma("b_zg", z_g.t[:], ZG.t[ch * 128:(ch + 1) * 128, :], R=[ZG], W=[z_g])
                        yield
                        S.op("pool", lambda e: e.tensor_tensor(gz.t[:], z_g.t[:], ggain.t[:], op=ALU.mult), R=[z_g, ggain], W=[gz])
                        for g in range(2):
                            pap = psa.ap(7)

                            def im(e):
                                last = None
                                for h2 in range(2):
                                    hh = g * 2 + h2
                                    last = e.matmul(pap[:, h2 * 256:(h2 + 1) * 256], q_b.t[:, hh * 128:(hh + 1) * 128],
                                                    Sb_bf.t[:, hh * 256:(hh + 1) * 256], start=True, stop=True)
                                return last
                            S.op("pe", im, R=[q_b, Sb_bf], W=[b7])
                            S.op("dve", lambda e: e.scalar_tensor_tensor(
                                osum.t[:, g * 512:(g + 1) * 512], pap, QS, op_.t[:, g * 512:(g + 1) * 512],
                                op0=ALU.mult, op1=ALU.add), R=[b7, op_], W=[osum])
                            yield
                        for hh in range(4):
                            S.op("dve", lambda e: e.scalar_tensor_tensor(
                                Sb.t[:, hh * 256:(hh + 1) * 256], Sb.t[:, hh * 256:(hh + 1) * 256], d_b.t[:, hh:hh + 1],
                                ub_.t[:, hh * 256:(hh + 1) * 256], op0=ALU.mult, op1=ALU.add), R=[Sb, d_b, ub_], W=[Sb])
                        S.op("pool", lambda e: e.tensor_copy(Sb_bf.t[:], Sb.t[:]), R=[Sb], W=[Sb_bf])
                        yield
                        for hh in range(4):
                            S.op("dve", lambda e: e.tensor_tensor(jk.t[:], osum.t[:, hh * 256:(hh + 1) * 256],
                                                                  osum.t[:, hh * 256:(hh + 1) * 256], op=ALU.mult), R=[osum], W=[jk])
                            S.op("dve", lambda e: e.reduce_sum(ssg.t[:, hh:hh + 1], jk.t[:], axis=AX.X), R=[jk], W=[ssg])
                        yield
                        S.op("act", lambda e: e.activation(ssg.t[:], ssg.t[:], AF.Ln, scale=1.0 / 256, bias=EPS), R=[ssg], W=[ssg])
                        S.op("act", lambda e: e.activation(ssg.t[:], ssg.t[:], AF.Exp, scale=-0.5), R=[ssg], W=[ssg])
                        yield
                        for hh in range(4):
                            S.op("dve", lambda e: e.scalar_tensor_tensor(
                                og_t.t[:, hh * 256:(hh + 1) * 256], osum.t[:, hh * 256:(hh + 1) * 256], ssg.t[:, hh:hh + 1],
                                gz.t[:, hh * 256:(hh + 1) * 256], op0=ALU.mult, op1=ALU.mult), R=[osum, ssg, gz], W=[og_t])
                        S.dma("b_og", OG.t[ch * 128:(ch + 1) * 128, :], og_t.t[:], R=[og_t], W=[OG])
                        yield

            sw2 = sweep2()
            steps = []
            heads = []
            for ji, (Tq, Tk, q0, KTd, Vd) in enumerate(((TS, TS, 0, KTs, Vs), (TO, TP, TS, KTp, Vp))):
                for h in range(8):
                    heads.append((Tq, Tk, q0, KTd, Vd, h))
            for hidx, (Tq, Tk, q0, KTd, Vd, h) in enumerate(heads):
                nb = Tk // 128
                for qt in range(Tq // 512):
                    for kb in range(nb):
                        steps.append((hidx, qt, kb, nb))
            loaded = set()

            def load_head(hidx):
                if hidx in loaded or hidx >= len(heads):
                    return
                loaded.add(hidx)
                Tq, Tk, q0, KTd, Vd, h = heads[hidx]
                sl = hidx % 2
                S.dma("c_k%d" % sl, KTt[sl].t[:, 0:Tk], KTd.t[h], R=[KTd], W=[KTt[sl]])
                S.dma("c_v%d" % sl, Vt[sl].t[:, 0:Tk // 128, :], Vd.t[h], R=[Vd], W=[Vt[sl]])
                S.dma("c_q%d" % sl, QTt[sl].t[:, 0:Tq], QT.t[h, :, q0:q0 + Tq], R=[QT], W=[QTt[sl]])

            def qk(si):
                hidx, qt, kb, nb = steps[si]
                load_head(hidx)
                sl = hidx % 2
                kt, qt_ = KTt[sl], QTt[sl]
                sb0 = (si % 2) * 2

                def f(e):
                    last = None
                    for z in range(2):
                        last = e.matmul(psa.ap(sb0 + z), kt.t[z * 64:(z + 1) * 64, kb * 128:(kb + 1) * 128],
                                        qt_.t[z * 64:(z + 1) * 64, qt * 512:(qt + 1) * 512],
                                        start=True, stop=True, tile_position=(z * 64, 0))
                    return last
                S.op("pe", f, R=[kt, qt_], W=[sc_t[si % 2]])
                E = Et[si % 3]
                S.op("act", lambda e: e.activation(E.t[:], psa.t[:, sb0 * 512:(sb0 + 2) * 512], AF.Exp, scale=0.125),
                     R=[sc_t[si % 2]], W=[E])

            def pv(si):
                hidx, qt, kb, nb = steps[si]
                vt = Vt[hidx % 2]
                E = Et[si % 3]

                for bk in range(3):
                    def f(e):
                        last = None
                        for a in range(bk * 3, min(bk * 3 + 3, 8)):
                            z, qb = a // 4, a % 4
                            off = (4 + a // 3) * 512 + (a % 3) * 129
                            last = e.matmul(psa.t[:, off:off + 129], E.t[:, z * 512 + qb * 128:z * 512 + (qb + 1) * 128],
                                            vt.t[:, kb, :], start=(kb == 0 and a % 3 == 0), stop=(kb == nb - 1),
                                            skip_group_check=True)
                        return last
                    S.op("pe", f, R=[E, vt], W=[acc_t[bk]])

            epi_i = [0]

            def epilogue(si):
                hidx, qt, kb, nb = steps[si]
                Tq, Tk, q0, KTd, Vd, h = heads[hidx]
                ei_ = epi_i[0]
                ac = accs[ei_ % 2]
                od_t = odt[ei_ % 2]
                o_o, ss4 = o_os[ei_ % 2], ss4s[ei_ % 2]
                epi_i[0] += 1
                for bk in range(3):
                    na = 3 if bk < 2 else 2
                    S.op("dve", lambda e: e.tensor_copy(
                        ac.t[:, bk * 3:bk * 3 + na, :],
                        psa.t[:, (4 + bk) * 512:(4 + bk) * 512 + na * 129].rearrange("p (a c) -> p a c", c=129)),
                        R=[acc_t[bk]], W=[ac])
                S.op("dve", lambda e: e.reciprocal(rr8.t[:], ac.t[:, :, 128]), R=[ac], W=[rr8])
                S.op("dve", lambda e: e.tensor_scalar(rl4.t[:], rr8.t[:, 4:8], nlam.t[:, 0:1], None, op0=ALU.mult), R=[rr8, nlam], W=[rl4])
                for qb in range(4):
                    S.op("dve", lambda e: e.tensor_scalar(t_o.t[:], ac.t[:, qb, 0:128], rr8.t[:, qb:qb + 1], None, op0=ALU.mult),
                         R=[ac, rr8], W=[t_o])
                    S.op("dve", lambda e: e.scalar_tensor_tensor(
                        o_o.t[:, qb, :], ac.t[:, 4 + qb, 0:128], rl4.t[:, qb:qb + 1], t_o.t[:], op0=ALU.mult, op1=ALU.add),
                        R=[ac, rl4, t_o], W=[o_o])
                    S.op("dve", lambda e: e.tensor_tensor(jk2.t[:], o_o.t[:, qb, :], o_o.t[:, qb, :], op=ALU.mult), R=[o_o], W=[jk2])
                    S.op("dve", lambda e: e.reduce_sum(ss4.t[:, qb:qb + 1], jk2.t[:], axis=AX.X), R=[jk2], W=[ss4])
                pending.append((si + 6, lambda: epilogue2(si, od_t, ei_)))

            def epilogue2(si, od_t, ei_):
                hidx, qt, kb, nb = steps[si]
                Tq, Tk, q0, KTd, Vd, h = heads[hidx]
                o_o, ss4 = o_os[ei_ % 2], ss4s[ei_ % 2]
                S.op("act", lambda e: e.activation(ss4.t[:], ss4.t[:], AF.Ln, scale=1.0 / 128, bias=EPS), R=[ss4], W=[ss4])
                S.op("act", lambda e: e.activation(ss4.t[:], ss4.t[:], AF.Exp, scale=-0.5), R=[ss4], W=[ss4])
                for qb in range(4):
                    S.op("dve", lambda e: e.scalar_tensor_tensor(
                        od_t.t[:, qb, :], o_o.t[:, qb, :], ss4.t[:, qb:qb + 1], gd.t[:], op0=ALU.mult, op1=ALU.mult),
                        R=[o_o, ss4, gd], W=[od_t])
                r0 = q0 + qt * 512
                S.dma("c_od%d" % (ei_ % 2),
                      OD.t[r0:r0 + 512, h * 128:(h + 1) * 128].rearrange("(qb p) e -> p qb e", p=128), od_t.t[:], R=[od_t], W=[OD])

            load_head(0)
            load_head(1)
            ns = len(steps)
            qk(0)
            qk(1)
            sw_alive = [True]

            def sw_step():
                if sw_alive[0]:
                    try:
                        next(sw2)
                    except StopIteration:
                        sw_alive[0] = False

            for si in range(ns):
                if si + 2 < ns:
                    qk(si + 2)
                pv(si)
                while pending and pending[0][0] <= si:
                    pending.pop(0)[1]()
                if si % 8 == 4 and steps[si][2] + 8 < steps[si][3]:
                    sw_step()
                hidx, qt, kb, nb = steps[si]
                if kb == nb - 1:
                    epilogue(si)
                    if qt == heads[hidx][0] // 512 - 1:
                        load_head(hidx + 2)
            while pending:
                pending.pop(0)[1]()
            while sw_alive[0]:
                sw_step()
            S.barrier()

        with ExitStack() as st:
            psf = Banks(nc, st, "psf3", 6, F32)
            psb = Banks(nc, st, "psb3", 2, BF16)
            Wt = {}
            wspec = (("g", w_bog, 0, 1024), ("d", w_bod, 0, 1024), ("o", w_outd, 0, 1024), ("m", w_in, C_MG, 2048))
            for nm, src, c0, ncol in wspec:
                Wt[nm] = sb(st, "W_" + nm, [128, 8, ncol], BF16)
            with ExitStack() as stw:
                stg = [sb(stw, "wstg2%d" % i, [128, 2048], F32) for i in range(2)]
                i = 0
                for nm, src, c0, ncol in wspec:
                    for kc in range(8):
                        g = stg[i % 2]
                        S.dma("wstg2%d" % (i % 2), g.t[:, 0:ncol], src[kc * 128:(kc + 1) * 128, c0:c0 + ncol], W=[g])
                        eng = ["dve", "pool"][i % 2]
                        S.op(eng, lambda e: e.tensor_copy(Wt[nm].t[:, kc, :], g.t[:, 0:ncol]), R=[g], W=[Wt[nm]])
                        i += 1
                S.barrier()
            fg = sb(st, "fg", [128, 1024], F32)
            S.dma("c_fg", fg.t[:], fgain.partition_broadcast(128), W=[fg])
            gate_bc = sb(st, "gate_bc2", [128, 2, 1024], F32)
            S.dma("c_gate", gate_bc.t[:].rearrange("p s c -> p (s c)"), GATE.t, R=[GATE], W=[gate_bc])
            xt = [sb(st, "xd%d" % i, [128, 1024], F32) for i in range(3)]
            hTt = [sb(st, "hd%d" % i, [128, 8, 128], BF16) for i in range(3)]
            ogl = [sb(st, "ogl%d" % i, [128, 1024], BF16) for i in range(3)]
            odl = [sb(st, "odl%d" % i, [128, 1024], BF16) for i in range(3)]
            zdl = [sb(st, "zdl%d" % i, [128, 1024], BF16) for i in range(3)]
            odg = sb(st, "odg", [128, 1024], BF16)
            oT = [sb(st, "oT%d" % i, [128, 8, 128], BF16) for i in range(2)]
            gm = sb(st, "gm", [128, 2048], F32)
            ta = sb(st, "ta", [128, 1024], F32)
            tb_ = sb(st, "tb", [128, 1024], F32)
            mg = sb(st, "mg", [128, 1024], BF16)
            mT = sb(st, "mT", [128, 8, 128], BF16)
            r_ = sb(st, "r_", [128, 1024], F32)
            jk3 = sb(st, "jk3", [128, 1024], BF16)
            s3 = sb(st, "s3", [128, 2], F32)
            yt = [sb(st, "yt%d" % i, [128, 1024], F32) for i in range(2)]

            def transpose8(src, dst):
                b, bap, bt = psb.next()

                def tr(e):
                    last = None
                    for k in range(8):
                        last = e.transpose(bap[:, k * 128:(k + 1) * 128], src.t[:, k * 128:(k + 1) * 128], identb.t[:])
                    return last
                S.op("pe", tr, R=[src, identb], W=[bt])
                S.op("act", lambda e: e.copy(dst.t[:].rearrange("p k t -> p (k t)"), bap), R=[bt], W=[dst])

            def mm_tm(lhs, wname, c0):
                b, pap, pt = psf.next()

                def f(e):
                    last = None
                    for kc in range(8):
                        last = e.matmul(pap, lhs.t[:, kc, :], Wt[wname].t[:, kc, c0:c0 + 512], start=(kc == 0), stop=(kc == 7))
                    return last
                S.op("pe", f, R=[lhs, Wt[wname]], W=[pt])
                return pap, pt

            mgs = [mg, sb(st, "mg1", [128, 1024], BF16)]
            tc = sb(st, "tc", [128, 1024], F32)

            def srcdst(ch):
                s = 0 if ch < TS // 128 else 1
                if s == 0:
                    return s, xs[ch * 128:(ch + 1) * 128, :], y_s[ch * 128:(ch + 1) * 128, :]
                c2 = ch - TS // 128
                return s, xo[c2 * 128:(c2 + 1) * 128, :], y_o[c2 * 128:(c2 + 1) * 128, :]

            def L(ch):
                sl = ch % 3
                s, xsrc, ydst = srcdst(ch)
                x, hh_, og_, od_, zd_ = xt[sl], hTt[sl], ogl[sl], odl[sl], zdl[sl]
                S.dma("d_x%d" % sl, x.t[:], xsrc, W=[x])
                S.dma("d_h%d" % sl, hh_.t[:].rearrange("p k t -> p (k t)"), HT.t[ch], R=[HT], W=[hh_])
                S.dma("d_og%d" % sl, og_.t[:], OG.t[ch * 128:(ch + 1) * 128, :], R=[OG], W=[og_])
                S.dma("d_od%d" % sl, od_.t[:], OD.t[ch * 128:(ch + 1) * 128, :], R=[OD], W=[od_])
                S.dma("d_zd%d" % sl, zd_.t[:], ZD.t[ch * 128:(ch + 1) * 128, :], R=[ZD], W=[zd_])

            def A1(ch):
                sl = ch % 3
                x, hh_, og_, od_, zd_ = xt[sl], hTt[sl], ogl[sl], odl[sl], zdl[sl]
                S.op("pool", lambda e: e.tensor_tensor(odg.t[:], od_.t[:], zd_.t[:], op=ALU.mult), R=[od_, zd_], W=[odg])
                transpose8(og_, oT[0])
                for g in range(4):
                    pap, pt = mm_tm(hh_, "m", g * 512)
                    S.op("act", lambda e: e.activation(gm.t[:, g * 512:(g + 1) * 512], pap, AF.Sigmoid), R=[pt], W=[gm])
                transpose8(odg, oT[1])

            def A2a(ch):
                for g in range(2):
                    pg, ptg = mm_tm(oT[0], "g", g * 512)
                    S.op("dve", lambda e: e.tensor_tensor(ta.t[:, g * 512:(g + 1) * 512], pg, gm.t[:, g * 512:(g + 1) * 512], op=ALU.mult),
                         R=[ptg, gm], W=[ta])

            def A2b(ch):
                for g in range(2):
                    pd, ptd = mm_tm(oT[1], "d", g * 512)
                    S.op("dve", lambda e: e.tensor_tensor(tb_.t[:, g * 512:(g + 1) * 512], pd, gm.t[:, 1024 + g * 512:1024 + (g + 1) * 512], op=ALU.mult),
                         R=[ptd, gm], W=[tb_])
                m_ = mgs[ch % 2]
                S.op("dve", lambda e: e.tensor_tensor(m_.t[:], ta.t[:], tb_.t[:], op=ALU.add), R=[ta, tb_], W=[m_])

            def B1(ch):
                transpose8(mgs[ch % 2], mT)

            def B2(ch):
                s, xsrc, ydst = srcdst(ch)
                for g in range(2):
                    po, pto = mm_tm(mT, "o", g * 512)
                    S.op("dve", lambda e: e.tensor_tensor(tc.t[:, g * 512:(g + 1) * 512], po, gate_bc.t[:, s, g * 512:(g + 1) * 512], op=ALU.mult),
                         R=[pto, gate_bc], W=[tc])

            def B3(ch):
                sl = ch % 2
                s, xsrc, ydst = srcdst(ch)
                x = xt[ch % 3]
                S.op("dve", lambda e: e.tensor_tensor(r_.t[:], tc.t[:], x.t[:], op=ALU.add), R=[tc, x], W=[r_])
                S.op("act", lambda e: e.activation(jk3.t[:], r_.t[:], AF.Square, accum_out=s3.t[:, 0:1]), R=[r_], W=[jk3, s3])
                S.op("act", lambda e: e.activation(s3.t[:, 1:2], s3.t[:, 0:1], AF.Ln, scale=1.0 / D, bias=EPS), R=[s3], W=[s3])
                S.op("act", lambda e: e.activation(s3.t[:, 1:2], s3.t[:, 1:2], AF.Exp, scale=-0.5), R=[s3], W=[s3])
                y = yt[sl]
                S.op("dve", lambda e: e.scalar_tensor_tensor(y.t[:], r_.t[:], s3.t[:, 1:2], fg.t[:], op0=ALU.mult, op1=ALU.mult),
                     R=[r_, s3, fg], W=[y])
                S.dma("d_y%d" % sl, ydst, y.t[:], R=[y])

            L(0)
            L(1)
            A1(0)
            A2a(0)
            A2b(0)
            for ch in range(NCH):
                nxt = ch + 1 < NCH
                if ch + 2 < NCH:
                    L(ch + 2)
                if nxt:
                    A1(ch + 1)
                B1(ch)
                if nxt:
                    A2a(ch + 1)
                B2(ch)
                if nxt:
                    A2b(ch + 1)
                B3(ch)
            S.barrier()
    return nc


_CACHE = {}


def _consts():
    if "c" in _CACHE:
        return _CACHE["c"]
    i = np.arange(128)
    t_, i_ = i[:, None], i[None, :]
    neg = np.float32(-1.0 / 16.0)
    tri = np.stack([
        (t_ <= i_), (t_ >= i_), (t_ > i_), (t_ < i_)]).astype(np.float32) * neg
    msk = np.stack([(t_ <= i_), (t_ >= i_)]).astype(np.float32)
    negs = np.full((128, 1), neg, np.float32)
    half = 32
    inv = (1.0 / (np.float32(10000.0) ** (np.arange(half, dtype=np.float32) / np.float32(half)))).astype(np.float32)
    ang = (np.arange(TP, dtype=np.float32)[:, None] * inv[None, :]).astype(np.float32)
    rope = np.concatenate([np.cos(ang), np.sin(ang)], axis=1).astype(np.float32)
    _CACHE["c"] = dict(tri=tri, msk=msk, negs=negs, rope=rope, ident=np.eye(128, dtype=np.float32))
    return _CACHE["c"]


def kernel(x_prompt, x_sample, c_prompt, c_sample, w_ada, b_ada, norm_gain, w_in,
           w_alpha, b_alpha, gla_norm_gain, lambda_q, lambda_k, diff_norm_gain,
           w_bo_gla, w_bo_diff, w_out, final_gain):
    f = lambda a: np.ascontiguousarray(np.asarray(a, dtype=np.float32))
    x_prompt, x_sample, c_prompt, c_sample = f(x_prompt), f(x_sample), f(c_prompt), f(c_sample)
    w_ada_, b_ada_, ng, w_in_ = f(w_ada)[0], f(b_ada)[0], f(norm_gain)[0], f(w_in)[0]
    wal, bal = f(w_alpha)[0], f(b_alpha)[0]
    C = _consts()
    walpha = np.zeros((33, 1024), np.float32)
    walpha[0:16, 0:512] = wal[0]
    walpha[16:32, 512:1024] = wal[1]
    walpha[32, 0:512] = bal[0]
    walpha[32, 512:1024] = bal[1]
    shared = dict(
        w_ada=w_ada_, b_adaT=np.ascontiguousarray(b_ada_.reshape(24, 128).T),
        b_gate=np.ascontiguousarray(b_ada_[2048:3072].reshape(1, D)),
        ngT=np.ascontiguousarray(ng.reshape(8, 128).T), w_in=w_in_, walpha=walpha,
        gla_gain=np.ascontiguousarray(np.tile(f(gla_norm_gain)[0], 4).reshape(1, 1024)),
        lqk=np.ascontiguousarray(np.concatenate([f(lambda_q)[0].reshape(-1), f(lambda_k)[0].reshape(-1)]).reshape(1, 256)),
        diff_gain=f(diff_norm_gain)[0].reshape(1, 128),
        w_bog=f(w_bo_gla)[0], w_bod=f(w_bo_diff)[0], w_out=f(w_out)[0],
        fgain=f(final_gain).reshape(1, D), ident=C["ident"], tri=C["tri"], msk=C["msk"], negs=C["negs"],
        rope_t=C["rope"],
    )
    in_maps = []
    for c in range(NC_):
        m = dict(shared)
        m["xs"] = x_sample[c]
        m["xo"] = np.ascontiguousarray(x_prompt[0, c * TO:(c + 1) * TO])
        cT = np.stack([c_sample[c].reshape(8, 128).T, c_prompt[0].reshape(8, 128).T], axis=-1)
        m["cT"] = np.ascontiguousarray(cT.astype(np.float32))
        m["rope_o"] = np.ascontiguousarray(C["rope"][c * TO:(c + 1) * TO])
        m["xr"] = np.ascontiguousarray(np.concatenate([x_prompt[0, :c * TO], x_prompt[0, (c + 1) * TO:]], axis=0))
        m["rope_r"] = np.ascontiguousarray(np.concatenate([C["rope"][:c * TO], C["rope"][(c + 1) * TO:]], axis=0))
        n = np.arange(128)
        mf = (n < 16 * c).astype(np.float32)
        mb = ((n >= 16 * c) & (n < 112)).astype(np.float32)
        m["segm"] = np.concatenate([mf, 1 - mf, mb, 1 - mb]).reshape(1, 512).astype(np.float32)
        in_maps.append(m)
    if "nc" not in _CACHE:
        _CACHE["nc"] = build()
    res = run_bass_kernel_spmd(_CACHE["nc"], in_maps, core_ids=list(range(NC_)))
    y_prompt = np.concatenate([res.results[c]["y_o"] for c in range(NC_)], axis=0)[None]
    y_sample = np.stack([res.results[c]["y_s"] for c in range(NC_)], axis=0)
    return (y_prompt.astype(np.float32), y_sample.astype(np.float32))
```

```python
import math
from contextlib import ExitStack

import numpy as np
import concourse.bass as bass
import concourse.mybir as mybir
from concourse.bass_utils import run_bass_kernel_spmd

F32 = mybir.dt.float32
BF16 = mybir.dt.bfloat16
AF = mybir.ActivationFunctionType
ALU = mybir.AluOpType
AX = mybir.AxisListType

D = 1024
NC_ = 8
TS = 4096
TO = 2048
TP = 16384
TL = TS + TO
NCH = TL // 128
WCOLS = 7200
EPS = 1e-6
LAM_INIT = 0.8 - 0.6 * math.exp(0.0)
QS = 128.0 ** -0.5

C_AQ, C_AK, C_AV, C_AZ, C_LOW, C_DQ, C_DK, C_DV, C_DZ, C_MG, C_MD = (
    0, 512, 1024, 2048, 3072, 3104, 4128, 5152, 6176, 7200, 8224)


class Obj:
    __slots__ = ("lw", "lr")

    def __init__(self):
        self.lw = {}
        self.lr = {}


class Tl:
    def __init__(self, t):
        self.t = t
        self.o = Obj()


class Sync:
    ENG = ["pe", "act", "dve", "pool", "sp"]

    def __init__(self, nc, stack):
        self.nc = nc
        self.stack = stack
        self.e = {"pe": nc.tensor, "act": nc.scalar, "dve": nc.vector, "pool": nc.gpsimd, "sp": nc.sync}
        self.semobj = {}
        self.cnt = {}
        self.known = {k: {} for k in self.ENG}
        for k in self.ENG:
            self.newsem(k)

    def newsem(self, key):
        self.semobj[key] = self.stack.enter_context(self.nc.semaphore("s_" + key))
        self.cnt[key] = 0

    def _waits(self, eng, reads, writes, extra=None):
        w = {}
        for t in reads:
            for k, v in t.o.lw.items():
                if w.get(k, 0) < v:
                    w[k] = v
        for t in writes:
            for d in (t.o.lw, t.o.lr):
                for k, v in d.items():
                    if w.get(k, 0) < v:
                        w[k] = v
        if extra:
            for k, v in extra.items():
                if w.get(k, 0) < v:
                    w[k] = v
        kn = self.known[eng]
        for k, v in w.items():
            if kn.get(k, 0) < v:
                self.e[eng].wait_ge(self.semobj[k], v)
                kn[k] = v

    def op(self, eng, fn, R=(), W=()):
        self._waits(eng, R, W)
        inst = fn(self.e[eng])
        self.cnt[eng] += 1
        inst.then_inc(self.semobj[eng], 1)
        v = self.cnt[eng]
        for t in R:
            t.o.lr[eng] = v
        for t in W:
            t.o.lw[eng] = v

    def dma(self, key, out, in_, R=(), W=(), q="sp"):
        if key not in self.semobj:
            self.newsem(key)
        self._waits(q, R, W, extra={key: self.cnt[key]})
        inst = self.e[q].dma_start(out=out, in_=in_)
        self.cnt[key] += 16
        inst.then_inc(self.semobj[key], 16)
        v = self.cnt[key]
        for t in R:
            t.o.lr[key] = v
        for t in W:
            t.o.lw[key] = v

    def barrier(self, engs=None):
        allv = dict(self.cnt)
        for eng in (engs or self.ENG):
            kn = self.known[eng]
            for k, v in allv.items():
                if v > 0 and kn.get(k, 0) < v:
                    self.e[eng].wait_ge(self.semobj[k], v)
                    kn[k] = v


class Banks:
    def __init__(self, nc, stack, name, nbanks, dt):
        per = 512 if dt == F32 else 1024
        self.per = per
        self.t = stack.enter_context(nc.psum_tensor(name, [128, nbanks * per], dt))
        self.tl = [Tl(self.t) for _ in range(nbanks)]
        self.n = nbanks
        self.i = 0

    def ap(self, b):
        return self.t[:, b * self.per:(b + 1) * self.per]

    def next(self):
        b = self.i
        self.i = (self.i + 1) % self.n
        return b, self.ap(b), self.tl[b]


def build():
    nc = bass.Bass("TRN2", target_bir_lowering=False)

    def din(name, shape, dt=F32):
        return nc.dram_tensor(name, list(shape), dt, kind="ExternalInput").ap()

    def dscr(name, shape, dt):
        return Tl(nc.dram_tensor(name, list(shape), dt, kind="Internal").ap())

    xs = din("xs", [TS, D]); xo = din("xo", [TO, D]); xr = din("xr", [TP - TO, D])
    cT = din("cT", [128, 8, 2])
    w_ada = din("w_ada", [D, 3 * D]); b_adaT = din("b_adaT", [128, 24]); b_gate = din("b_gate", [1, D])
    ngT = din("ngT", [128, 8])
    w_in = din("w_in", [D, 9248])
    walpha = din("walpha", [33, 1024])
    gla_gain = din("gla_gain", [1, 1024])
    lqk = din("lqk", [1, 256])
    diff_gain = din("diff_gain", [1, 128])
    w_bog = din("w_bog", [D, D]); w_bod = din("w_bod", [D, D]); w_outd = din("w_out", [D, D])
    fgain = din("fgain", [1, D])
    ident_d = din("ident", [128, 128])
    tri_d = din("tri", [4, 128, 128])
    msk_d = din("msk", [2, 128, 128])
    negs_d = din("negs", [128, 1])
    rope_t = din("rope_t", [TP, 64])
    rope_o = din("rope_o", [TO, 64])
    rope_r = din("rope_r", [TP - TO, 64])
    segm = din("segm", [1, 512])
    y_s = nc.dram_tensor("y_s", [TS, D], F32, kind="ExternalOutput").ap()
    y_o = nc.dram_tensor("y_o", [TO, D], F32, kind="ExternalOutput").ap()

    QT = dscr("QT", [8, 128, TL], BF16)
    KTs = dscr("KTs", [8, 128, TS], BF16); KTp = dscr("KTp", [8, 128, TP], BF16)
    Vs = dscr("Vs", [8, 128, TS // 128, 129], BF16); Vp = dscr("Vp", [8, 128, TP // 128, 129], BF16)
    HT = dscr("HT", [NCH, 128, 1024], BF16)
    ZG = dscr("ZG", [TL, D], BF16); ZD = dscr("ZD", [TL, D], BF16)
    OG = dscr("OG", [TL, D], BF16); OD = dscr("OD", [TL, D], BF16)
    OPART = dscr("OPART", [NCH, 128, 1024], F32)
    QBT = dscr("QBT", [NCH, 128, 512], BF16)
    UB = dscr("UB", [NCH, 128, 1024], F32)
    DECB = dscr("DECB", [NCH, 128, 4], F32)
    GATE = dscr("GATE", [128, 2048], F32)

    with ExitStack() as st0:
        S = Sync(nc, st0)

        def sb(stack, name, shape, dt):
            return Tl(stack.enter_context(nc.sbuf_tensor("sb_" + name, list(shape), dt)))

        ident = sb(st0, "ident", [128, 128], F32)
        identb = sb(st0, "identb", [128, 128], BF16)
        Aco = sb(st0, "Aco", [128, 2, 8], F32)
        Bco = sb(st0, "Bco", [128, 2, 8], F32)
        nlam = sb(st0, "nlam", [128, 1], F32)
        gd = sb(st0, "gd", [128, 128], F32)
        Sb_in = sb(st0, "Sb_in", [128, 1024], F32)
        Sf = sb(st0, "Sf", [128, 1024], F32)
        S.dma("c_id", ident.t[:], ident_d, W=[ident])
        S.op("act", lambda e: e.copy(identb.t[:], ident.t[:]), R=[ident], W=[identb])

        with ExitStack() as st:
            ps = Banks(nc, st, "ps0", 6, F32)
            c_sb = sb(st, "c_sb", [128, 8, 2], F32)
            gate_bc = sb(st, "gate_bc", [128, 2, 1024], F32)
            sc = sb(st, "sc", [128, 8, 2], F32)
            sc_bc = sb(st, "sc_bc", [128, 2, 8, 128], F32)
            wa = [sb(st, "wa%d" % i, [128, 3072], F32) for i in range(2)]
            badT = sb(st, "badT", [128, 24], F32)
            ngs = sb(st, "ngs", [128, 8], F32)
            bg = sb(st, "bg", [128, 1024], F32)
            lq = sb(st, "lq", [128, 256], F32)
            pr = sb(st, "pr", [128, 128], F32)
            e2 = sb(st, "e2", [128, 2], F32)
            tmp8 = sb(st, "tmp8", [128, 8], F32)
            dg = sb(st, "dg", [128, 128], F32)
            S.dma("p0a", c_sb.t[:], cT, W=[c_sb])
            S.dma("p0b", badT.t[:], b_adaT, W=[badT])
            S.dma("p0c", ngs.t[:], ngT, W=[ngs])
            S.dma("p0d", bg.t[:], b_gate.partition_broadcast(128), W=[bg])
            S.dma("p0e", lq.t[:], lqk.partition_broadcast(128), W=[lq])
            S.dma("p0f", dg.t[:], diff_gain.partition_broadcast(128), W=[dg])
            S.op("act", lambda e: e.activation(sc.t[:], c_sb.t[:], AF.Silu), R=[c_sb], W=[sc])
            for s in range(2):
                S.op("dve", lambda e: e.tensor_copy(
                    sc_bc.t[:, s], sc.t[:, :, s:s + 1].to_broadcast([128, 8, 128])), R=[sc], W=[sc_bc])
            _, modp, modt = ps.next()
            gps = [ps.next() for _ in range(4)]
            for kc in range(8):
                w = wa[kc % 2]
                S.dma("wa%d" % (kc % 2), w.t[:], w_ada[kc * 128:(kc + 1) * 128, :], W=[w])

                def mm(e):
                    last = None
                    for j in range(16):
                        last = e.matmul(modp[:, j * 2:(j + 1) * 2], w.t[:, j * 128:(j + 1) * 128],
                                        sc.t[:, kc, :], start=(kc == 0 and j == 0), stop=(kc == 7),
                                        skip_group_check=True)
                    return last
                S.op("pe", mm, R=[w, sc], W=[modt])
                for s in range(2):
                    for hf in range(2):
                        _, gp, gt = gps[s * 2 + hf]
                        S.op("pe", lambda e: e.matmul(
                            gp, sc_bc.t[:, s, kc, :], w.t[:, 2048 + hf * 512:2048 + (hf + 1) * 512],
                            start=(kc == 0), stop=(kc == 7)), R=[w, sc_bc], W=[gt])
            modv = modp[:, 0:32].rearrange("p (j s) -> p j s", s=2)
            for s in range(2):
                S.op("dve", lambda e: e.tensor_tensor(Bco.t[:, s, :], modv[:, 0:8, s], badT.t[:, 0:8], op=ALU.add),
                     R=[modt, badT], W=[Bco])
                S.op("dve", lambda e: e.tensor_tensor(tmp8.t[:], modv[:, 8:16, s], badT.t[:, 8:16], op=ALU.add),
                     R=[modt, badT], W=[tmp8])
                S.op("dve", lambda e: e.scalar_tensor_tensor(
                    Aco.t[:, s, :], tmp8.t[:], 1.0, ngs.t[:], op0=ALU.add, op1=ALU.mult),
                    R=[tmp8, ngs], W=[Aco])
                for hf in range(2):
                    _, gp, gt = gps[s * 2 + hf]
                    S.op("dve", lambda e: e.tensor_tensor(
                        gate_bc.t[:, s, hf * 512:(hf + 1) * 512], gp, bg.t[:, hf * 512:(hf + 1) * 512], op=ALU.add),
                        R=[gt, bg], W=[gate_bc])
            S.dma("st_gate", GATE.t, gate_bc.t[:].rearrange("p s c -> p (s c)"), R=[gate_bc], W=[GATE])
            S.op("dve", lambda e: e.tensor_tensor(pr.t[:], lq.t[:, 0:128], lq.t[:, 128:256], op=ALU.mult), R=[lq], W=[pr])
            S.op("dve", lambda e: e.tensor_reduce(
                e2.t[:], pr.t[:].rearrange("p (z d) -> p z d", z=2), axis=AX.X, op=ALU.add), R=[pr], W=[e2])
            S.op("act", lambda e: e.activation(e2.t[:], e2.t[:], AF.Exp), R=[e2], W=[e2])
            S.op("dve", lambda e: e.tensor_tensor(nlam.t[:], e2.t[:, 1:2], e2.t[:, 0:1], op=ALU.subtract), R=[e2], W=[nlam])
            S.op("dve", lambda e: e.tensor_scalar(nlam.t[:], nlam.t[:], -LAM_INIT, None, op0=ALU.add), R=[nlam], W=[nlam])
            S.op("dve", lambda e: e.tensor_scalar(gd.t[:], dg.t[:], 1.0 - LAM_INIT, None, op0=ALU.mult), R=[dg], W=[gd])
            S.barrier()

        with ExitStack() as st:
            W = sb(st, "Wres", [128, 8, WCOLS], BF16)
            with ExitStack() as stw:
                stg = [sb(stw, "wstg%d" % i, [128, 2400], F32) for i in range(2)]
                i = 0
                for kc in range(8):
                    for c0 in range(0, WCOLS, 2400):
                        g = stg[i % 2]
                        S.dma("wstg%d" % (i % 2), g.t[:], w_in[kc * 128:(kc + 1) * 128, c0:c0 + 2400], W=[g])
                        eng = ["dve", "pool", "act"][i % 3]
                        if eng == "act":
                            S.op("act", lambda e: e.copy(W.t[:, kc, c0:c0 + 2400], g.t[:]), R=[g], W=[W])
                        else:
                            S.op(eng, lambda e: e.tensor_copy(W.t[:, kc, c0:c0 + 2400], g.t[:]), R=[g], W=[W])
                        i += 1
                S.barrier()
            psf = Banks(nc, st, "psf", 6, F32)
            psb = Banks(nc, st, "psb", 2, BF16)
            tri = sb(st, "tri", [128, 4, 128], F32)
            msk = sb(st, "msk", [128, 2, 128], F32)
            negs = sb(st, "negs", [128, 1], F32)
            wal = sb(st, "wal", [33, 1024], F32)
            seg = sb(st, "seg", [128, 512], F32)
            lowT = sb(st, "lowT", [33, 128], F32)
            S.dma("c_tri", tri.t[:], tri_d.rearrange("k p i -> p k i"), W=[tri])
            S.dma("c_msk", msk.t[:], msk_d.rearrange("k p i -> p k i"), W=[msk])
            S.dma("c_neg", negs.t[:], negs_d, W=[negs])
            S.dma("c_wal", wal.t[:], walpha, W=[wal])
            S.dma("c_seg", seg.t[:], segm.partition_broadcast(128), W=[seg])
            S.op("dve", lambda e: e.memset(lowT.t[32:33, :], 1.0), W=[lowT])
            xt = [sb(st, "xt%d" % i, [128, 1024], F32) for i in range(2)]
            cs = [sb(st, "cs%d" % i, [128, 64], F32) for i in range(2)]
            junk = sb(st, "junk", [128, 1024], BF16)
            st2 = sb(st, "st2", [128, 2], F32)
            xn = sb(st, "xn", [128, 1024], F32)
            hT = [sb(st, "hT%d" % i, [128, 8, 128], BF16) for i in range(2)]
            lsps = [sb(st, "lsp%d" % i, [128, 1024], F32) for i in range(2)]
            Ea = sb(st, "Ea", [128, 512], F32)
            Eb = sb(st, "Eb", [128, 512], F32)
            qfT = sb(st, "qfT", [128, 512], BF16); kfT = sb(st, "kfT", [128, 512], BF16)
            qbT = sb(st, "qbT", [128, 512], BF16); kbT = sb(st, "kbT", [128, 512], BF16)
            khf = sb(st, "khf", [128, 512], BF16); khb = sb(st, "khb", [128, 512], BF16)
            dec = sb(st, "dec", [128, 8], F32)
            vtm = sb(st, "vtm", [128, 1024], BF16)
            m1 = sb(st, "m1", [128, 512], F32); m2 = sb(st, "m2", [128, 512], F32)
            PT = sb(st, "PT", [128, 512], BF16)
            oev = sb(st, "oev", [128, 1024], F32)
            ubev = sb(st, "ubev", [128, 1024], F32)
            Sf_bf = sb(st, "Sf_bf", [128, 1024], BF16)
            Dp = sb(st, "Dp", [128, 4], F32)
            co4 = sb(st, "co4", [128, 8], F32)
            zst = [sb(st, "zst%d" % i, [128, 1024], BF16) for i in range(2)]
            rt = [sb(st, "rt%d" % i, [128, 8, 32], F32) for i in range(4)]
            rq = sb(st, "rq", [128, 1024], BF16)
            qkT = [sb(st, "qkT%d" % i, [128, 8, 128], BF16) for i in range(2)]
            vaug = sb(st, "vaug", [128, 8, 129], BF16)
            S.op("pool", lambda e: e.memset(vaug.t[:], 1.0), W=[vaug])
            S.op("pool", lambda e: e.memset(Sf.t[:], 0.0), W=[Sf])
            S.op("pool", lambda e: e.memset(Sb_in.t[:], 0.0), W=[Sb_in])
            S.op("pool", lambda e: e.memset(Sf_bf.t[:], 0.0), W=[Sf_bf])
            S.op("pool", lambda e: e.memset(Dp.t[:], 1.0), W=[Dp])

            def proj_tm(hslot, c0, n=512):
                b, pap, pt = psf.next()

                def f(e):
                    last = None
                    for kc in range(8):
                        last = e.matmul(pap[:, 0:n], hslot.t[:, kc, :], W.t[:, kc, c0:c0 + n],
                                        start=(kc == 0), stop=(kc == 7))
                    return last
                S.op("pe", f, R=[hslot, W], W=[pt])
                return pap, pt

            def proj_fm(hslot, c0, ngrp, m=128):
                b, pap, pt = psf.next()

                def f(e):
                    last = None
                    for g in range(ngrp):
                        for kc in range(8):
                            last = e.matmul(pap[0:m, g * 128:(g + 1) * 128], W.t[:, kc, c0 + g * m:c0 + (g + 1) * m],
                                            hslot.t[:, kc, :], start=(kc == 0), stop=(kc == 7))
                    return last
                S.op("pe", f, R=[hslot, W], W=[pt])
                return pap, pt

            def rope_store(hslot, c0, cst, dstT, t0, slot):
                for g in range(2):
                    pap, pt = proj_tm(hslot, c0 + g * 512)
                    xv = pap.rearrange("p (b d) -> p b d", d=64)
                    cosb = cst.t[:, 0:32].unsqueeze(1).to_broadcast([128, 8, 32])
                    sinb = cst.t[:, 32:64].unsqueeze(1).to_broadcast([128, 8, 32])
                    S.op("dve", lambda e: e.tensor_tensor(rt[0].t[:], xv[:, :, 0:32], cosb, op=ALU.mult), R=[pt, cst], W=[rt[0]])
                    S.op("dve", lambda e: e.tensor_tensor(rt[1].t[:], xv[:, :, 32:64], sinb, op=ALU.mult), R=[pt, cst], W=[rt[1]])
                    S.op("dve", lambda e: e.tensor_tensor(rt[2].t[:], xv[:, :, 0:32], sinb, op=ALU.mult), R=[pt, cst], W=[rt[2]])
                    S.op("dve", lambda e: e.tensor_tensor(rt[3].t[:], xv[:, :, 32:64], cosb, op=ALU.mult), R=[pt, cst], W=[rt[3]])
                    rv = rq.t[:, g * 512:(g + 1) * 512].rearrange("p (b d) -> p b d", d=64)
                    S.op("pool", lambda e: e.tensor_tensor(rv[:, :, 0:32], rt[0].t[:], rt[1].t[:], op=ALU.subtract),
                         R=[rt[0], rt[1]], W=[rq])
                    S.op("pool", lambda e: e.tensor_tensor(rv[:, :, 32:64], rt[2].t[:], rt[3].t[:], op=ALU.add),
                         R=[rt[2], rt[3]], W=[rq])
                    yield
                b, bap, bt = psb.next()

                def tr(e):
                    last = None
                    for h in range(8):
                        last = e.transpose(bap[:, h * 128:(h + 1) * 128], rq.t[:, h * 128:(h + 1) * 128], identb.t[:])
                    return last
                S.op("pe", tr, R=[rq, identb], W=[bt])
                dst = qkT[slot]
                S.op("act", lambda e: e.copy(dst.t[:].rearrange("p h t -> p (h t)"), bap), R=[bt], W=[dst])
                S.dma("st_qk%d" % slot, dstT.t.rearrange("h p t -> p h t")[:, :, t0:t0 + 128], dst.t[:], R=[dst], W=[dstT])
                yield

            def front_a(td):
                it, xrows, ropesrc, s, mode, lt0, kv_dst, pre = td
                sl = it % 2
                x = xt[sl]; c_ = cs[sl]
                S.dma("ldx%d" % sl, x.t[:], xrows, W=[x])
                S.dma("ldc%d" % sl, c_.t[:], ropesrc, W=[c_])
                S.op("act", lambda e: e.activation(junk.t[:], x.t[:], AF.Square, accum_out=st2.t[:, 0:1]), R=[x], W=[junk, st2])
                S.op("act", lambda e: e.activation(st2.t[:, 1:2], st2.t[:, 0:1], AF.Ln, scale=1.0 / D, bias=EPS), R=[st2], W=[st2])
                S.op("act", lambda e: e.activation(st2.t[:, 1:2], st2.t[:, 1:2], AF.Exp, scale=-0.5), R=[st2], W=[st2])
                S.op("dve", lambda e: e.tensor_scalar(xn.t[:], x.t[:], st2.t[:, 1:2], None, op0=ALU.mult), R=[x, st2], W=[xn])

            def front_b(td):
                it, xrows, ropesrc, s, mode, lt0, kv_dst, pre = td
                sl = it % 2
                h = hT[sl]; lsp = lsps[sl]
                for g in range(2):
                    _, pap, pt = psf.next()

                    def tr(e):
                        last = None
                        for k4 in range(4):
                            kc = g * 4 + k4
                            last = e.transpose(pap[:, k4 * 128:(k4 + 1) * 128], xn.t[:, kc * 128:(kc + 1) * 128], ident.t[:])
                        return last
                    S.op("pe", tr, R=[xn, ident], W=[pt])
                    for k4 in range(4):
                        kc = g * 4 + k4
                        if k4 % 2 == 0:
                            S.op("act", lambda e: e.activation(
                                h.t[:, kc, :], pap[:, k4 * 128:(k4 + 1) * 128], AF.Identity,
                                scale=Aco.t[:, s, kc:kc + 1], bias=Bco.t[:, s, kc:kc + 1]), R=[pt, Aco, Bco], W=[h])
                        else:
                            S.op("dve", lambda e: e.tensor_scalar(
                                h.t[:, kc, :], pap[:, k4 * 128:(k4 + 1) * 128], Aco.t[:, s, kc:kc + 1],
                                Bco.t[:, s, kc:kc + 1], op0=ALU.mult, op1=ALU.add), R=[pt, Aco, Bco], W=[h])
                    yield
                if mode == "full":
                    S.dma("st_ht", HT.t[lt0 // 128], h.t[:].rearrange("p k t -> p (k t)"), R=[h], W=[HT])
                lp, lpt = proj_fm(h, C_LOW, 1, m=32)
                S.op("act", lambda e: e.copy(lowT.t[0:32, :], lp[0:32, 0:128]), R=[lpt], W=[lowT])
                yield
                for z in range(2):
                    _, pap, pt = psf.next()
                    S.op("pe", lambda e: e.matmul(pap, lowT.t[:], wal.t[:, z * 512:(z + 1) * 512], start=True, stop=True),
                         R=[lowT, wal], W=[pt])
                    S.op("act", lambda e: e.activation(lsp.t[:, z * 512:(z + 1) * 512], pap, AF.Exp, scale=-1.0), R=[pt], W=[lsp])
                S.op("act", lambda e: e.activation(lsp.t[:], lsp.t[:], AF.Ln, bias=1.0), R=[lsp], W=[lsp])
                yield

            def chain(td):
                it, xrows, ropesrc, s, mode, lt0, kv_dst, pre = td
                sl = it % 2
                h = hT[sl]; lsp = lsps[sl]
                ch = lt0 // 128 if mode == "full" else None
                if pre is not None:
                    pre()
                if mode == "full":
                    aqp, aqt = proj_fm(h, C_AQ, 4)
                    akp, akt = proj_fm(h, C_AK, 4)
                    for dr in range(2):
                        _, cp, ct = psf.next()

                        def cm(e):
                            last = None
                            for hh in range(4):
                                last = e.matmul(cp[:, hh * 128:(hh + 1) * 128],
                                                lsp.t[:, dr * 512 + hh * 128:dr * 512 + (hh + 1) * 128],
                                                tri.t[:, dr, :], start=True, stop=True)
                            return last
                        S.op("pe", cm, R=[lsp, tri], W=[ct])
                        S.op("act", lambda e: e.activation(Ea.t[:], cp, AF.Exp), R=[ct], W=[Ea])
                        S.op("act", lambda e: e.activation(Eb.t[:], cp, AF.Exp, scale=-1.0), R=[ct], W=[Eb])
                        qd, kd = (qfT, kfT) if dr == 0 else (qbT, kbT)
                        S.op("dve", lambda e: e.tensor_tensor(qd.t[:], aqp, Ea.t[:], op=ALU.mult), R=[aqt, Ea], W=[qd])
                        S.op("dve", lambda e: e.tensor_tensor(kd.t[:], akp, Eb.t[:], op=ALU.mult), R=[akt, Eb], W=[kd])
                        yield
                ktp, ktt = proj_tm(h, C_AK)
                for dr in range(2):
                    _, cp, ct = psf.next()
                    S.op("pe", lambda e: e.matmul(cp, tri.t[:, 2 + dr, :], lsp.t[:, dr * 512:(dr + 1) * 512], start=True, stop=True),
                         R=[lsp, tri], W=[ct])
                    Ex = Ea if dr == 0 else Eb
                    S.op("act", lambda e: e.activation(Ex.t[:], cp, AF.Exp), R=[ct], W=[Ex])
                    kh = khf if dr == 0 else khb
                    S.op("dve", lambda e: e.tensor_tensor(kh.t[:], ktp, Ex.t[:], op=ALU.mult), R=[ktt, Ex], W=[kh])
                    yield
                _, tp_, tt_ = psf.next()

                def tot(e):
                    last = None
                    for j in range(8):
                        last = e.matmul(tp_[:, j:j + 1], lsp.t[:, j * 128:(j + 1) * 128], negs.t[:], start=True, stop=True)
                    return last
                S.op("pe", tot, R=[lsp, negs], W=[tt_])
                S.op("act", lambda e: e.activation(dec.t[:], tp_[:, 0:8], AF.Exp), R=[tt_], W=[dec])
                for g in range(2):
                    pap, pt = proj_tm(h, C_AV + g * 512)
                    S.op("act", lambda e: e.copy(vtm.t[:, g * 512:(g + 1) * 512], pap), R=[pt], W=[vtm])
                yield
                if mode == "full":
                    sp_ = []
                    for dr in range(2):
                        _, pap, pt = psf.next()
                        qd, kd = (qfT, kfT) if dr == 0 else (qbT, kbT)

                        def scm(e):
                            last = None
                            for hh in range(4):
                                last = e.matmul(pap[:, hh * 128:(hh + 1) * 128], kd.t[:, hh * 128:(hh + 1) * 128],
                                                qd.t[:, hh * 128:(hh + 1) * 128], start=True, stop=True)
                            return last
                        S.op("pe", scm, R=[qd, kd], W=[pt])
                        sp_.append((pap, pt))
                    for dr in range(2):
                        pap, pt = sp_[dr]
                        mm_ = m1 if dr == 0 else m2
                        S.op("dve", lambda e: e.tensor_tensor(
                            mm_.t[:].rearrange("p (h i) -> p h i", h=4), pap.rearrange("p (h i) -> p h i", h=4),
                            msk.t[:, dr, :].unsqueeze(1).to_broadcast([128, 4, 128]), op=ALU.mult), R=[pt, msk], W=[mm_])
                    S.op("pool", lambda e: e.tensor_tensor(PT.t[:], m1.t[:], m2.t[:], op=ALU.add), R=[m1, m2], W=[PT])
                    yield
                    for g in range(2):
                        _, pap, pt = psf.next()

                        def om(e):
                            last = None
                            for h2 in range(2):
                                hh = g * 2 + h2
                                e.matmul(pap[:, h2 * 256:(h2 + 1) * 256], PT.t[:, hh * 128:(hh + 1) * 128],
                                         vtm.t[:, hh * 256:(hh + 1) * 256], start=True, stop=False)
                                last = e.matmul(pap[:, h2 * 256:(h2 + 1) * 256], qfT.t[:, hh * 128:(hh + 1) * 128],
                                                Sf_bf.t[:, hh * 256:(hh + 1) * 256], start=False, stop=True)
                            return last
                        S.op("pe", om, R=[PT, vtm, qfT, Sf_bf], W=[pt])
                        S.op("act", lambda e: e.activation(oev.t[:, g * 512:(g + 1) * 512], pap, AF.Copy, scale=QS), R=[pt], W=[oev])
                    S.dma("st_op", OPART.t[ch], oev.t[:], R=[oev], W=[OPART])
                    S.dma("st_qb", QBT.t[ch], qbT.t[:], R=[qbT], W=[QBT])
                    S.dma("st_db", DECB.t[ch], dec.t[:, 4:8], R=[dec], W=[DECB])
                    yield
                for dr in range(2):
                    kh = khf if dr == 0 else khb
                    ups = []
                    for g in range(2):
                        _, pap, pt = psf.next()

                        def um(e):
                            last = None
                            for h2 in range(2):
                                hh = g * 2 + h2
                                last = e.matmul(pap[:, h2 * 256:(h2 + 1) * 256], kh.t[:, hh * 128:(hh + 1) * 128],
                                                vtm.t[:, hh * 256:(hh + 1) * 256], start=True, stop=True)
                            return last
                        S.op("pe", um, R=[kh, vtm], W=[pt])
                        ups.append((pap, pt))
                    if mode == "full":
                        if dr == 0:
                            for hh in range(4):
                                pap, pt = ups[hh // 2]
                                S.op("dve", lambda e: e.scalar_tensor_tensor(
                                    Sf.t[:, hh * 256:(hh + 1) * 256], Sf.t[:, hh * 256:(hh + 1) * 256], dec.t[:, hh:hh + 1],
                                    pap[:, (hh % 2) * 256:(hh % 2 + 1) * 256], op0=ALU.mult, op1=ALU.add),
                                    R=[Sf, dec, pt], W=[Sf])
                            S.op("pool", lambda e: e.tensor_copy(Sf_bf.t[:], Sf.t[:]), R=[Sf], W=[Sf_bf])
                        else:
                            for g in range(2):
                                pap, pt = ups[g]
                                S.op("act", lambda e: e.copy(ubev.t[:, g * 512:(g + 1) * 512], pap), R=[pt], W=[ubev])
                            S.dma("st_ub", UB.t[ch], ubev.t[:], R=[ubev], W=[UB])
                    else:
                        n = it
                        if dr == 0:
                            S.op("dve", lambda e: e.scalar_tensor_tensor(
                                co4.t[:, 0:4], dec.t[:, 0:4], seg.t[:, n:n + 1],
                                seg.t[:, 128 + n:129 + n].to_broadcast([128, 4]), op0=ALU.mult, op1=ALU.add),
                                R=[dec, seg], W=[co4])
                            for g in range(2):
                                pap, pt = ups[g]
                                S.op("act", lambda e: e.activation(ubev.t[:, g * 512:(g + 1) * 512], pap, AF.Copy,
                                                                   scale=seg.t[:, n:n + 1]), R=[pt, seg], W=[ubev])
                            for hh in range(4):
                                S.op("dve", lambda e: e.scalar_tensor_tensor(
                                    Sf.t[:, hh * 256:(hh + 1) * 256], Sf.t[:, hh * 256:(hh + 1) * 256], co4.t[:, hh:hh + 1],
                                    ubev.t[:, hh * 256:(hh + 1) * 256], op0=ALU.mult, op1=ALU.add),
                                    R=[Sf, co4, ubev], W=[Sf])
                        else:
                            S.op("dve", lambda e: e.tensor_scalar(co4.t[:, 4:8], Dp.t[:], seg.t[:, 256 + n:257 + n], None, op0=ALU.mult),
                                 R=[Dp, seg], W=[co4])
                            for hh in range(4):
                                pap, pt = ups[hh // 2]
                                S.op("dve", lambda e: e.scalar_tensor_tensor(
                                    Sb_in.t[:, hh * 256:(hh + 1) * 256], pap[:, (hh % 2) * 256:(hh % 2 + 1) * 256],
                                    co4.t[:, 4 + hh:5 + hh], Sb_in.t[:, hh * 256:(hh + 1) * 256], op0=ALU.mult, op1=ALU.add),
                                    R=[Sb_in, co4, pt], W=[Sb_in])
                            S.op("dve", lambda e: e.scalar_tensor_tensor(
                                co4.t[:, 4:8], dec.t[:, 4:8], seg.t[:, 256 + n:257 + n],
                                seg.t[:, 384 + n:385 + n].to_broadcast([128, 4]), op0=ALU.mult, op1=ALU.add),
                                R=[dec, seg], W=[co4])
                            S.op("dve", lambda e: e.tensor_tensor(Dp.t[:], Dp.t[:], co4.t[:, 4:8], op=ALU.mult), R=[Dp, co4], W=[Dp])
                    yield

            def bulk(td):
                it, xrows, ropesrc, s, mode, lt0, kv_dst, pre = td
                sl = it % 2
                h = hT[sl]; c_ = cs[sl]
                if kv_dst is not None:
                    KTd, Vd, t0 = kv_dst
                    yield from rope_store(h, C_DK, c_, KTd, t0, 1)
                    for g in range(2):
                        pap, pt = proj_tm(h, C_DV + g * 512)
                        S.op("act", lambda e: e.copy(vaug.t[:, g * 4:(g + 1) * 4, 0:128],
                                                     pap.rearrange("p (h e) -> p h e", e=128)), R=[pt], W=[vaug])
                    S.dma("st_v", Vd.t[:, :, t0 // 128, :].rearrange("h p e -> p h e"), vaug.t[:], R=[vaug], W=[Vd])
                    yield
                if mode == "full":
                    yield from rope_store(h, C_DQ, c_, QT, lt0, 0)
                    for zi, (c0, dstz) in enumerate(((C_AZ, ZG), (C_DZ, ZD))):
                        zt = zst[zi]
                        for g in range(2):
                            pap, pt = proj_tm(h, c0 + g * 512)
                            S.op("act", lambda e: e.activation(zt.t[:, g * 512:(g + 1) * 512], pap, AF.Silu), R=[pt], W=[zt])
                        S.dma("st_z%d" % zi, dstz.t[lt0:lt0 + 128, :], zt.t[:], R=[zt], W=[dstz])
                        yield

            def interleave(gens):
                gens = list(gens)
                while gens:
                    for g in list(gens):
                        try:
                            next(g)
                        except StopIteration:
                            gens.remove(g)

            def pre_own():
                S.op("pool", lambda e: e.tensor_copy(Sf_bf.t[:], Sf.t[:]), R=[Sf], W=[Sf_bf])

            def pre_sample():
                S.op("pool", lambda e: e.memset(Sf.t[:], 0.0), W=[Sf])
                S.op("pool", lambda e: e.memset(Sf_bf.t[:], 0.0), W=[Sf_bf])

            tds = []
            for n in range((TP - TO) // 128):
                tds.append((n, xr[n * 128:(n + 1) * 128, :], rope_r[n * 128:(n + 1) * 128, :], 1, "kv", None,
                            (KTp, Vp, TO + n * 128), None))
            for n in range(TO // 128):
                tds.append((n, xo[n * 128:(n + 1) * 128, :], rope_o[n * 128:(n + 1) * 128, :], 1, "full", TS + n * 128,
                            (KTp, Vp, n * 128), pre_own if n == 0 else None))
            for n in range(TS // 128):
                tds.append((n, xs[n * 128:(n + 1) * 128, :], rope_t[n * 128:(n + 1) * 128, :], 0, "full", n * 128, (KTs, Vs, n * 128),
                            pre_sample if n == 0 else None))
            front_a(tds[0])
            interleave([front_b(tds[0])])
            for i, td in enumerate(tds):
                gens = [chain(td), bulk(td)]
                if i + 1 < len(tds):
                    front_a(tds[i + 1])
                    gens.append(front_b(tds[i + 1]))
                interleave(gens)
            S.barrier()

        with ExitStack() as st:
            psa = Banks(nc, st, "psa", 8, F32)
            KTt = [sb(st, "KTt%d" % i, [128, TP], BF16) for i in range(2)]
            Vt = [sb(st, "Vt%d" % i, [128, TP // 128, 129], BF16) for i in range(2)]
            QTt = [sb(st, "QTt%d" % i, [128, TS], BF16) for i in range(2)]
            Et = [sb(st, "Et%d" % i, [128, 1024], BF16) for i in range(3)]
            accs = [sb(st, "accs0", [128, 8, 129], F32)] * 2
            rr8 = sb(st, "rr8", [128, 8], F32)
            rl4 = sb(st, "rl4", [128, 4], F32)
            t_o = sb(st, "t_o", [128, 128], F32)
            o_os = [sb(st, "o_o%d" % i, [128, 4, 128], F32) for i in range(2)]
            ss4s = [sb(st, "ss4c%d" % i, [128, 4], F32) for i in range(2)]
            pending = []
            jk2 = sb(st, "jk2", [128, 128], F32)
            odt = [sb(st, "odt%d" % i, [128, 4, 128], BF16) for i in range(2)]
            sc_t = [Tl(None), Tl(None)]
            acc_t = [Tl(None), Tl(None), Tl(None)]
            ggain = sb(st, "ggain", [128, 1024], F32)
            S.dma("c_gg", ggain.t[:], gla_gain.partition_broadcast(128), W=[ggain])
            Sb = sb(st, "Sb", [128, 1024], F32)
            Sb_bf = sb(st, "Sb_bf", [128, 1024], BF16)
            op_ = sb(st, "op_", [128, 1024], F32)
            qb_ = [sb(st, "qb%d" % i, [128, 512], BF16) for i in range(2)]
            ub_ = sb(st, "ub_", [128, 1024], F32)
            db_ = [sb(st, "db%d" % i, [128, 4], F32) for i in range(2)]
            zg_ = [sb(st, "zg0", [128, 1024], BF16)] * 2
            osum = sb(st, "osum", [128, 1024], F32)
            gz = sb(st, "gz", [128, 1024], F32)
            jk = sb(st, "jk", [128, 256], F32)
            ssg = sb(st, "ssg", [128, 4], F32)
            ogt = [sb(st, "ogt0", [128, 1024], BF16)] * 2
            b7 = Tl(None)

            def sweep2():
                it = 0
                for (c_lo, c_hi, init_from) in ((0, TS // 128, None), (TS // 128, NCH, Sb_in)):
                    if init_from is None:
                        S.op("pool", lambda e: e.memset(Sb.t[:], 0.0), W=[Sb])
                    else:
                        S.op("pool", lambda e: e.tensor_copy(Sb.t[:], init_from.t[:]), R=[init_from], W=[Sb])
                    S.op("pool", lambda e: e.tensor_copy(Sb_bf.t[:], Sb.t[:]), R=[Sb], W=[Sb_bf])
                    for ch in range(c_hi - 1, c_lo - 1, -1):
                        sl = it % 2
                        it += 1
                        q_b, d_b, z_g, og_t = qb_[sl], db_[sl], zg_[sl], ogt[sl]
                        S.dma("b_op", op_.t[:], OPART.t[ch], R=[OPART], W=[op_])
                        S.dma("b_qb%d" % sl, q_b.t[:], QBT.t[ch], R=[QBT], W=[q_b])
                        S.dma("b_ub", ub_.t[:], UB.t[ch], R=[UB], W=[ub_])
                        S.dma("b_db%d" % sl, d_b.t[:], DECB.t[ch], R=[DECB], W=[d_b])
                        S.dma("b_zg", z_g.t[:], ZG.t[ch * 128:(ch + 1) * 128, :], R=[ZG], W=[z_g])
                        yield
                        yield
                        S.op("pool", lambda e: e.tensor_tensor(gz.t[:], z_g.t[:], ggain.t[:], op=ALU.mult), R=[z_g, ggain], W=[gz])
                        for g in range(2):
                            pap = psa.ap(7)

                            def im(e):
                                last = None
                                for h2 in range(2):
                                    hh = g * 2 + h2
                                    last = e.matmul(pap[:, h2 * 256:(h2 + 1) * 256], q_b.t[:, hh * 128:(hh + 1) * 128],
                                                    Sb_bf.t[:, hh * 256:(hh + 1) * 256], start=True, stop=True)
                                return last
                            S.op("pe", im, R=[q_b, Sb_bf], W=[b7])
                            S.op("dve", lambda e: e.scalar_tensor_tensor(
                                osum.t[:, g * 512:(g + 1) * 512], pap, QS, op_.t[:, g * 512:(g + 1) * 512],
                                op0=ALU.mult, op1=ALU.add), R=[b7, op_], W=[osum])
                            yield
                        for hh in range(4):
                            S.op("dve", lambda e: e.scalar_tensor_tensor(
                                Sb.t[:, hh * 256:(hh + 1) * 256], Sb.t[:, hh * 256:(hh + 1) * 256], d_b.t[:, hh:hh + 1],
                                ub_.t[:, hh * 256:(hh + 1) * 256], op0=ALU.mult, op1=ALU.add), R=[Sb, d_b, ub_], W=[Sb])
                        S.op("pool", lambda e: e.tensor_copy(Sb_bf.t[:], Sb.t[:]), R=[Sb], W=[Sb_bf])
                        yield
                        for hh in range(4):
                            S.op("dve", lambda e: e.tensor_tensor(jk.t[:], osum.t[:, hh * 256:(hh + 1) * 256],
                                                                  osum.t[:, hh * 256:(hh + 1) * 256], op=ALU.mult), R=[osum], W=[jk])
                            S.op("dve", lambda e: e.reduce_sum(ssg.t[:, hh:hh + 1], jk.t[:], axis=AX.X), R=[jk], W=[ssg])
                        yield
                        S.op("act", lambda e: e.activation(ssg.t[:], ssg.t[:], AF.Ln, scale=1.0 / 256, bias=EPS), R=[ssg], W=[ssg])
                        S.op("act", lambda e: e.activation(ssg.t[:], ssg.t[:], AF.Exp, scale=-0.5), R=[ssg], W=[ssg])
                        yield
                        for hh in range(4):
                            S.op("dve", lambda e: e.scalar_tensor_tensor(
                                og_t.t[:, hh * 256:(hh + 1) * 256], osum.t[:, hh * 256:(hh + 1) * 256], ssg.t[:, hh:hh + 1],
                                gz.t[:, hh * 256:(hh + 1) * 256], op0=ALU.mult, op1=ALU.mult), R=[osum, ssg, gz], W=[og_t])
                        S.dma("b_og", OG.t[ch * 128:(ch + 1) * 128, :], og_t.t[:], R=[og_t], W=[OG])
                        yield

            sw2 = sweep2()
            steps = []
            heads = []
            for ji, (Tq, Tk, q0, KTd, Vd) in enumerate(((TS, TS, 0, KTs, Vs), (TO, TP, TS, KTp, Vp))):
                for h in range(8):
                    heads.append((Tq, Tk, q0, KTd, Vd, h))
            for hidx, (Tq, Tk, q0, KTd, Vd, h) in enumerate(heads):
                nb = Tk // 128
                for qt in range(Tq // 512):
                    for kb in range(nb):
                        steps.append((hidx, qt, kb, nb))
            loaded = set()

            def load_head(hidx):
                if hidx in loaded or hidx >= len(heads):
                    return
                loaded.add(hidx)
                Tq, Tk, q0, KTd, Vd, h = heads[hidx]
                sl = hidx % 2
                S.dma("c_k%d" % sl, KTt[sl].t[:, 0:Tk], KTd.t[h], R=[KTd], W=[KTt[sl]])
                S.dma("c_v%d" % sl, Vt[sl].t[:, 0:Tk // 128, :], Vd.t[h], R=[Vd], W=[Vt[sl]])
                S.dma("c_q%d" % sl, QTt[sl].t[:, 0:Tq], QT.t[h, :, q0:q0 + Tq], R=[QT], W=[QTt[sl]])

            def qk(si):
                hidx, qt, kb, nb = steps[si]
                load_head(hidx)
                sl = hidx % 2
                kt, qt_ = KTt[sl], QTt[sl]
                sb0 = (si % 2) * 2

                def f(e):
                    last = None
                    for z in range(2):
                        last = e.matmul(psa.ap(sb0 + z), kt.t[z * 64:(z + 1) * 64, kb * 128:(kb + 1) * 128],
                                        qt_.t[z * 64:(z + 1) * 64, qt * 512:(qt + 1) * 512],
                                        start=True, stop=True, tile_position=(z * 64, 0))
                    return last
                S.op("pe", f, R=[kt, qt_], W=[sc_t[si % 2]])
                E = Et[si % 3]
                S.op("act", lambda e: e.activation(E.t[:], psa.t[:, sb0 * 512:(sb0 + 2) * 512], AF.Exp, scale=0.125),
                     R=[sc_t[si % 2]], W=[E])

            def pv(si):
                hidx, qt, kb, nb = steps[si]
                vt = Vt[hidx % 2]
                E = Et[si % 3]

                for bk in range(3):
                    def f(e):
                        last = None
                        for a in range(bk * 3, min(bk * 3 + 3, 8)):
                            z, qb = a // 4, a % 4
                            off = (4 + a // 3) * 512 + (a % 3) * 129
                            last = e.matmul(psa.t[:, off:off + 129], E.t[:, z * 512 + qb * 128:z * 512 + (qb + 1) * 128],
                                            vt.t[:, kb, :], start=(kb == 0 and a % 3 == 0), stop=(kb == nb - 1),
                                            skip_group_check=True)
                        return last
                    S.op("pe", f, R=[E, vt], W=[acc_t[bk]])

            epi_i = [0]

            def epilogue(si):
                hidx, qt, kb, nb = steps[si]
                Tq, Tk, q0, KTd, Vd, h = heads[hidx]
                ei_ = epi_i[0]
                ac = accs[ei_ % 2]
                od_t = odt[ei_ % 2]
                o_o, ss4 = o_os[ei_ % 2], ss4s[ei_ % 2]
                epi_i[0] += 1
                for bk in range(3):
                    na = 3 if bk < 2 else 2
                    S.op("dve", lambda e: e.tensor_copy(
                        ac.t[:, bk * 3:bk * 3 + na, :],
                        psa.t[:, (4 + bk) * 512:(4 + bk) * 512 + na * 129].rearrange("p (a c) -> p a c", c=129)),
                        R=[acc_t[bk]], W=[ac])
                S.op("dve", lambda e: e.reciprocal(rr8.t[:], ac.t[:, :, 128]), R=[ac], W=[rr8])
                S.op("dve", lambda e: e.tensor_scalar(rl4.t[:], rr8.t[:, 4:8], nlam.t[:, 0:1], None, op0=ALU.mult), R=[rr8, nlam], W=[rl4])
                for qb in range(4):
                    S.op("dve", lambda e: e.tensor_scalar(t_o.t[:], ac.t[:, qb, 0:128], rr8.t[:, qb:qb + 1], None, op0=ALU.mult),
                         R=[ac, rr8], W=[t_o])
                    S.op("dve", lambda e: e.scalar_tensor_tensor(
                        o_o.t[:, qb, :], ac.t[:, 4 + qb, 0:128], rl4.t[:, qb:qb + 1], t_o.t[:], op0=ALU.mult, op1=ALU.add),
                        R=[ac, rl4, t_o], W=[o_o])
                    S.op("dve", lambda e: e.tensor_tensor(jk2.t[:], o_o.t[:, qb, :], o_o.t[:, qb, :], op=ALU.mult), R=[o_o], W=[jk2])
                    S.op("dve", lambda e: e.reduce_sum(ss4.t[:, qb:qb + 1], jk2.t[:], axis=AX.X), R=[jk2], W=[ss4])
                pending.append((si + 6, lambda: epilogue2(si, od_t, ei_)))

            def epilogue2(si, od_t, ei_):
                hidx, qt, kb, nb = steps[si]
                Tq, Tk, q0, KTd, Vd, h = heads[hidx]
                o_o, ss4 = o_os[ei_ % 2], ss4s[ei_ % 2]
                S.op("act", lambda e: e.activation(ss4.t[:], ss4.t[:], AF.Ln, scale=1.0 / 128, bias=EPS), R=[ss4], W=[ss4])
                S.op("act", lambda e: e.activation(ss4.t[:], ss4.t[:], AF.Exp, scale=-0.5), R=[ss4], W=[ss4])
                for qb in range(4):
                    S.op("dve", lambda e: e.scalar_tensor_tensor(
                        od_t.t[:, qb, :], o_o.t[:, qb, :], ss4.t[:, qb:qb + 1], gd.t[:], op0=ALU.mult, op1=ALU.mult),
                        R=[o_o, ss4, gd], W=[od_t])
                r0 = q0 + qt * 512
                S.dma("c_od%d" % (ei_ % 2),
                      OD.t[r0:r0 + 512, h * 128:(h + 1) * 128].rearrange("(qb p) e -> p qb e", p=128), od_t.t[:], R=[od_t], W=[OD])

            load_head(0)
            load_head(1)
            ns = len(steps)
            qk(0)
            qk(1)
            sw_alive = [True]

            def sw_step():
                if sw_alive[0]:
                    try:
                        next(sw2)
                    except StopIteration:
                        sw_alive[0] = False

            for si in range(ns):
                if si + 2 < ns:
                    qk(si + 2)
                pv(si)
                while pending and pending[0][0] <= si:
                    pending.pop(0)[1]()
                if si % 8 == 4 and steps[si][2] + 8 < steps[si][3]:
                    sw_step()
                hidx, qt, kb, nb = steps[si]
                if kb == nb - 1:
                    epilogue(si)
                    if qt == heads[hidx][0] // 512 - 1:
                        load_head(hidx + 2)
            while pending:
                pending.pop(0)[1]()
            while sw_alive[0]:
                sw_step()
            S.barrier()

        with ExitStack() as st:
            psf = Banks(nc, st, "psf3", 6, F32)
            psb = Banks(nc, st, "psb3", 2, BF16)
            Wt = {}
            wspec = (("g", w_bog, 0, 1024), ("d", w_bod, 0, 1024), ("o", w_outd, 0, 1024), ("m", w_in, C_MG, 2048))
            for nm, src, c0, ncol in wspec:
                Wt[nm] = sb(st, "W_" + nm, [128, 8, ncol], BF16)
            with ExitStack() as stw:
                stg = [sb(stw, "wstg2%d" % i, [128, 2048], F32) for i in range(2)]
                i = 0
                for nm, src, c0, ncol in wspec:
                    for kc in range(8):
                        g = stg[i % 2]
                        S.dma("wstg2%d" % (i % 2), g.t[:, 0:ncol], src[kc * 128:(kc + 1) * 128, c0:c0 + ncol], W=[g])
                        eng = ["dve", "pool"][i % 2]
                        S.op(eng, lambda e: e.tensor_copy(Wt[nm].t[:, kc, :], g.t[:, 0:ncol]), R=[g], W=[Wt[nm]])
                        i += 1
                S.barrier()
            fg = sb(st, "fg", [128, 1024], F32)
            S.dma("c_fg", fg.t[:], fgain.partition_broadcast(128), W=[fg])
            gate_bc = sb(st, "gate_bc2", [128, 2, 1024], F32)
            S.dma("c_gate", gate_bc.t[:].rearrange("p s c -> p (s c)"), GATE.t, R=[GATE], W=[gate_bc])
            xt = [sb(st, "xd%d" % i, [128, 1024], F32) for i in range(3)]
            hTt = [sb(st, "hd%d" % i, [128, 8, 128], BF16) for i in range(3)]
            ogl = [sb(st, "ogl%d" % i, [128, 1024], BF16) for i in range(3)]
            odl = [sb(st, "odl%d" % i, [128, 1024], BF16) for i in range(3)]
            zdl = [sb(st, "zdl%d" % i, [128, 1024], BF16) for i in range(3)]
            odg = sb(st, "odg", [128, 1024], BF16)
            oT = [sb(st, "oT%d" % i, [128, 8, 128], BF16) for i in range(2)]
            gm = sb(st, "gm", [128, 2048], F32)
            ta = sb(st, "ta", [128, 1024], F32)
            tb_ = sb(st, "tb", [128, 1024], F32)
            mg = sb(st, "mg", [128, 1024], BF16)
            mT = sb(st, "mT", [128, 8, 128], BF16)
            r_ = sb(st, "r_", [128, 1024], F32)
            jk3 = sb(st, "jk3", [128, 1024], BF16)
            s3 = sb(st, "s3", [128, 2], F32)
            yt = [sb(st, "yt%d" % i, [128, 1024], F32) for i in range(2)]

            def transpose8(src, dst):
                b, bap, bt = psb.next()

                def tr(e):
                    last = None
                    for k in range(8):
                        last = e.transpose(bap[:, k * 128:(k + 1) * 128], src.t[:, k * 128:(k + 1) * 128], identb.t[:])
                    return last
                S.op("pe", tr, R=[src, identb], W=[bt])
                S.op("act", lambda e: e.copy(dst.t[:].rearrange("p k t -> p (k t)"), bap), R=[bt], W=[dst])

            def mm_tm(lhs, wname, c0):
                b, pap, pt = psf.next()

                def f(e):
                    last = None
                    for kc in range(8):
                        last = e.matmul(pap, lhs.t[:, kc, :], Wt[wname].t[:, kc, c0:c0 + 512], start=(kc == 0), stop=(kc == 7))
                    return last
                S.op("pe", f, R=[lhs, Wt[wname]], W=[pt])
                return pap, pt

            mgs = [mg, sb(st, "mg1", [128, 1024], BF16)]
            tc = sb(st, "tc", [128, 1024], F32)

            def srcdst(ch):
                s = 0 if ch < TS // 128 else 1
                if s == 0:
                    return s, xs[ch * 128:(ch + 1) * 128, :], y_s[ch * 128:(ch + 1) * 128, :]
                c2 = ch - TS // 128
                return s, xo[c2 * 128:(c2 + 1) * 128, :], y_o[c2 * 128:(c2 + 1) * 128, :]

            def L(ch):
                sl = ch % 3
                s, xsrc, ydst = srcdst(ch)
                x, hh_, og_, od_, zd_ = xt[sl], hTt[sl], ogl[sl], odl[sl], zdl[sl]
                S.dma("d_x%d" % sl, x.t[:], xsrc, W=[x])
                S.dma("d_h%d" % sl, hh_.t[:].rearrange("p k t -> p (k t)"), HT.t[ch], R=[HT], W=[hh_])
                S.dma("d_og%d" % sl, og_.t[:], OG.t[ch * 128:(ch + 1) * 128, :], R=[OG], W=[og_])
                S.dma("d_od%d" % sl, od_.t[:], OD.t[ch * 128:(ch + 1) * 128, :], R=[OD], W=[od_])
                S.dma("d_zd%d" % sl, zd_.t[:], ZD.t[ch * 128:(ch + 1) * 128, :], R=[ZD], W=[zd_])

            def A1(ch):
                sl = ch % 3
                x, hh_, og_, od_, zd_ = xt[sl], hTt[sl], ogl[sl], odl[sl], zdl[sl]
                S.op("pool", lambda e: e.tensor_tensor(odg.t[:], od_.t[:], zd_.t[:], op=ALU.mult), R=[od_, zd_], W=[odg])
                transpose8(og_, oT[0])
                for g in range(4):
                    pap, pt = mm_tm(hh_, "m", g * 512)
                    S.op("act", lambda e: e.activation(gm.t[:, g * 512:(g + 1) * 512], pap, AF.Sigmoid), R=[pt], W=[gm])
                transpose8(odg, oT[1])

            def A2a(ch):
                for g in range(2):
                    pg, ptg = mm_tm(oT[0], "g", g * 512)
                    S.op("dve", lambda e: e.tensor_tensor(ta.t[:, g * 512:(g + 1) * 512], pg, gm.t[:, g * 512:(g + 1) * 512], op=ALU.mult),
                         R=[ptg, gm], W=[ta])

            def A2b(ch):
                for g in range(2):
                    pd, ptd = mm_tm(oT[1], "d", g * 512)
                    S.op("dve", lambda e: e.tensor_tensor(tb_.t[:, g * 512:(g + 1) * 512], pd, gm.t[:, 1024 + g * 512:1024 + (g + 1) * 512], op=ALU.mult),
                         R=[ptd, gm], W=[tb_])
                m_ = mgs[ch % 2]
                S.op("dve", lambda e: e.tensor_tensor(m_.t[:], ta.t[:], tb_.t[:], op=ALU.add), R=[ta, tb_], W=[m_])

            def B1(ch):
                transpose8(mgs[ch % 2], mT)

            def B2(ch):
                s, xsrc, ydst = srcdst(ch)
                for g in range(2):
                    po, pto = mm_tm(mT, "o", g * 512)
                    S.op("dve", lambda e: e.tensor_tensor(tc.t[:, g * 512:(g + 1) * 512], po, gate_bc.t[:, s, g * 512:(g + 1) * 512], op=ALU.mult),
                         R=[pto, gate_bc], W=[tc])

            def B3(ch):
                sl = ch % 2
                s, xsrc, ydst = srcdst(ch)
                x = xt[ch % 3]
                S.op("dve", lambda e: e.tensor_tensor(r_.t[:], tc.t[:], x.t[:], op=ALU.add), R=[tc, x], W=[r_])
                S.op("act", lambda e: e.activation(jk3.t[:], r_.t[:], AF.Square, accum_out=s3.t[:, 0:1]), R=[r_], W=[jk3, s3])
                S.op("act", lambda e: e.activation(s3.t[:, 1:2], s3.t[:, 0:1], AF.Ln, scale=1.0 / D, bias=EPS), R=[s3], W=[s3])
                S.op("act", lambda e: e.activation(s3.t[:, 1:2], s3.t[:, 1:2], AF.Exp, scale=-0.5), R=[s3], W=[s3])
                y = yt[sl]
                S.op("dve", lambda e: e.scalar_tensor_tensor(y.t[:], r_.t[:], s3.t[:, 1:2], fg.t[:], op0=ALU.mult, op1=ALU.mult),
                     R=[r_, s3, fg], W=[y])
                S.dma("d_y%d" % sl, ydst, y.t[:], R=[y])

            L(0)
            L(1)
            A1(0)
            A2a(0)
            A2b(0)
            for ch in range(NCH):
                nxt = ch + 1 < NCH
                if ch + 2 < NCH:
                    L(ch + 2)
                if nxt:
                    A1(ch + 1)
                B1(ch)
                if nxt:
                    A2a(ch + 1)
                B2(ch)
                if nxt:
                    A2b(ch + 1)
                B3(ch)
            S.barrier()
    return nc


_CACHE = {}


def _consts():
    if "c" in _CACHE:
        return _CACHE["c"]
    i = np.arange(128)
    t_, i_ = i[:, None], i[None, :]
    neg = np.float32(-1.0 / 16.0)
    tri = np.stack([
        (t_ <= i_), (t_ >= i_), (t_ > i_), (t_ < i_)]).astype(np.float32) * neg
    msk = np.stack([(t_ <= i_), (t_ >= i_)]).astype(np.float32)
    negs = np.full((128, 1), neg, np.float32)
    half = 32
    inv = (1.0 / (np.float32(10000.0) ** (np.arange(half, dtype=np.float32) / np.float32(half)))).astype(np.float32)
    ang = (np.arange(TP, dtype=np.float32)[:, None] * inv[None, :]).astype(np.float32)
    rope = np.concatenate([np.cos(ang), np.sin(ang)], axis=1).astype(np.float32)
    _CACHE["c"] = dict(tri=tri, msk=msk, negs=negs, rope=rope, ident=np.eye(128, dtype=np.float32))
    return _CACHE["c"]


def kernel(x_prompt, x_sample, c_prompt, c_sample, w_ada, b_ada, norm_gain, w_in,
           w_alpha, b_alpha, gla_norm_gain, lambda_q, lambda_k, diff_norm_gain,
           w_bo_gla, w_bo_diff, w_out, final_gain):
    f = lambda a: np.ascontiguousarray(np.asarray(a, dtype=np.float32))
    x_prompt, x_sample, c_prompt, c_sample = f(x_prompt), f(x_sample), f(c_prompt), f(c_sample)
    w_ada_, b_ada_, ng, w_in_ = f(w_ada)[0], f(b_ada)[0], f(norm_gain)[0], f(w_in)[0]
    wal, bal = f(w_alpha)[0], f(b_alpha)[0]
    C = _consts()
    walpha = np.zeros((33, 1024), np.float32)
    walpha[0:16, 0:512] = wal[0]
    walpha[16:32, 512:1024] = wal[1]
    walpha[32, 0:512] = bal[0]
    walpha[32, 512:1024] = bal[1]
    shared = dict(
        w_ada=w_ada_, b_adaT=np.ascontiguousarray(b_ada_.reshape(24, 128).T),
        b_gate=np.ascontiguousarray(b_ada_[2048:3072].reshape(1, D)),
        ngT=np.ascontiguousarray(ng.reshape(8, 128).T), w_in=w_in_, walpha=walpha,
        gla_gain=np.ascontiguousarray(np.tile(f(gla_norm_gain)[0], 4).reshape(1, 1024)),
        lqk=np.ascontiguousarray(np.concatenate([f(lambda_q)[0].reshape(-1), f(lambda_k)[0].reshape(-1)]).reshape(1, 256)),
        diff_gain=f(diff_norm_gain)[0].reshape(1, 128),
        w_bog=f(w_bo_gla)[0], w_bod=f(w_bo_diff)[0], w_out=f(w_out)[0],
        fgain=f(final_gain).reshape(1, D), ident=C["ident"], tri=C["tri"], msk=C["msk"], negs=C["negs"],
        rope_t=C["rope"],
    )
    in_maps = []
    for c in range(NC_):
        m = dict(shared)
        m["xs"] = x_sample[c]
        m["xo"] = np.ascontiguousarray(x_prompt[0, c * TO:(c + 1) * TO])
        cT = np.stack([c_sample[c].reshape(8, 128).T, c_prompt[0].reshape(8, 128).T], axis=-1)
        m["cT"] = np.ascontiguousarray(cT.astype(np.float32))
        m["rope_o"] = np.ascontiguousarray(C["rope"][c * TO:(c + 1) * TO])
        m["xr"] = np.ascontiguousarray(np.concatenate([x_prompt[0, :c * TO], x_prompt[0, (c + 1) * TO:]], axis=0))
        m["rope_r"] = np.ascontiguousarray(np.concatenate([C["rope"][:c * TO], C["rope"][(c + 1) * TO:]], axis=0))
        n = np.arange(128)
        mf = (n < 16 * c).astype(np.float32)
        mb = ((n >= 16 * c) & (n < 112)).astype(np.float32)
        m["segm"] = np.concatenate([mf, 1 - mf, mb, 1 - mb]).reshape(1, 512).astype(np.float32)
        in_maps.append(m)
    if "nc" not in _CACHE:
        _CACHE["nc"] = build()
    res = run_bass_kernel_spmd(_CACHE["nc"], in_maps, core_ids=list(range(NC_)))
    y_prompt = np.concatenate([res.results[c]["y_o"] for c in range(NC_)], axis=0)[None]
    y_sample = np.stack([res.results[c]["y_s"] for c in range(NC_)], axis=0)
    return (y_prompt.astype(np.float32), y_sample.astype(np.float32))
```

```python
import math
from contextlib import ExitStack

import numpy as np
import concourse.bass as bass
import concourse.mybir as mybir
from concourse.bass_utils import run_bass_kernel_spmd

F32 = mybir.dt.float32
BF16 = mybir.dt.bfloat16
AF = mybir.ActivationFunctionType
ALU = mybir.AluOpType
AX = mybir.AxisListType

D = 1024
NC_ = 8
TS = 4096
TO = 2048
TP = 16384
TL = TS + TO
NCH = TL // 128
WCOLS = 7200
EPS = 1e-6
LAM_INIT = 0.8 - 0.6 * math.exp(0.0)
QS = 128.0 ** -0.5

C_AQ, C_AK, C_AV, C_AZ, C_LOW, C_DQ, C_DK, C_DV, C_DZ, C_MG, C_MD = (
    0, 512, 1024, 2048, 3072, 3104, 4128, 5152, 6176, 7200, 8224)


class Obj:
    __slots__ = ("lw", "lr")

    def __init__(self):
        self.lw = {}
        self.lr = {}


class Tl:
    def __init__(self, t):
        self.t = t
        self.o = Obj()


class Sync:
    ENG = ["pe", "act", "dve", "pool", "sp"]

    def __init__(self, nc, stack):
        self.nc = nc
        self.stack = stack
        self.e = {"pe": nc.tensor, "act": nc.scalar, "dve": nc.vector, "pool": nc.gpsimd, "sp": nc.sync}
        self.semobj = {}
        self.cnt = {}
        self.known = {k: {} for k in self.ENG}
        for k in self.ENG:
            self.newsem(k)

    def newsem(self, key):
        self.semobj[key] = self.stack.enter_context(self.nc.semaphore("s_" + key))
        self.cnt[key] = 0

    def _waits(self, eng, reads, writes, extra=None):
        w = {}
        for t in reads:
            for k, v in t.o.lw.items():
                if w.get(k, 0) < v:
                    w[k] = v
        for t in writes:
            for d in (t.o.lw, t.o.lr):
                for k, v in d.items():
                    if w.get(k, 0) < v:
                        w[k] = v
        if extra:
            for k, v in extra.items():
                if w.get(k, 0) < v:
                    w[k] = v
        kn = self.known[eng]
        for k, v in w.items():
            if kn.get(k, 0) < v:
                self.e[eng].wait_ge(self.semobj[k], v)
                kn[k] = v

    def op(self, eng, fn, R=(), W=()):
        self._waits(eng, R, W)
        inst = fn(self.e[eng])
        self.cnt[eng] += 1
        inst.then_inc(self.semobj[eng], 1)
        v = self.cnt[eng]
        for t in R:
            t.o.lr[eng] = v
        for t in W:
            t.o.lw[eng] = v

    def dma(self, key, out, in_, R=(), W=(), q="sp"):
        if key not in self.semobj:
            self.newsem(key)
        self._waits(q, R, W, extra={key: self.cnt[key]})
        inst = self.e[q].dma_start(out=out, in_=in_)
        self.cnt[key] += 16
        inst.then_inc(self.semobj[key], 16)
        v = self.cnt[key]
        for t in R:
            t.o.lr[key] = v
        for t in W:
            t.o.lw[key] = v

    def barrier(self, engs=None):
        allv = dict(self.cnt)
        for eng in (engs or self.ENG):
            kn = self.known[eng]
            for k, v in allv.items():
                if v > 0 and kn.get(k, 0) < v:
                    self.e[eng].wait_ge(self.semobj[k], v)
                    kn[k] = v


class Banks:
    def __init__(self, nc, stack, name, nbanks, dt):
        per = 512 if dt == F32 else 1024
        self.per = per
        self.t = stack.enter_context(nc.psum_tensor(name, [128, nbanks * per], dt))
        self.tl = [Tl(self.t) for _ in range(nbanks)]
        self.n = nbanks
        self.i = 0

    def ap(self, b):
        return self.t[:, b * self.per:(b + 1) * self.per]

    def next(self):
        b = self.i
        self.i = (self.i + 1) % self.n
        return b, self.ap(b), self.tl[b]


def build():
    nc = bass.Bass("TRN2", target_bir_lowering=False)

    def din(name, shape, dt=F32):
        return nc.dram_tensor(name, list(shape), dt, kind="ExternalInput").ap()

    def dscr(name, shape, dt):
        return Tl(nc.dram_tensor(name, list(shape), dt, kind="Internal").ap())

    xs = din("xs", [TS, D]); xo = din("xo", [TO, D]); xr = din("xr", [TP - TO, D])
    cT = din("cT", [128, 8, 2])
    w_ada = din("w_ada", [D, 3 * D]); b_adaT = din("b_adaT", [128, 24]); b_gate = din("b_gate", [1, D])
    ngT = din("ngT", [128, 8])
    w_in = din("w_in", [D, 9248])
    walpha = din("walpha", [33, 1024])
    gla_gain = din("gla_gain", [1, 1024])
    lqk = din("lqk", [1, 256])
    diff_gain = din("diff_gain", [1, 128])
    w_bog = din("w_bog", [D, D]); w_bod = din("w_bod", [D, D]); w_outd = din("w_out", [D, D])
    fgain = din("fgain", [1, D])
    ident_d = din("ident", [128, 128])
    tri_d = din("tri", [4, 128, 128])
    msk_d = din("msk", [2, 128, 128])
    negs_d = din("negs", [128, 1])
    rope_t = din("rope_t", [TP, 64])
    rope_o = din("rope_o", [TO, 64])
    rope_r = din("rope_r", [TP - TO, 64])
    segm = din("segm", [1, 512])
    y_s = nc.dram_tensor("y_s", [TS, D], F32, kind="ExternalOutput").ap()
    y_o = nc.dram_tensor("y_o", [TO, D], F32, kind="ExternalOutput").ap()

    QT = dscr("QT", [8, 128, TL], BF16)
    KTs = dscr("KTs", [8, 128, TS], BF16); KTp = dscr("KTp", [8, 128, TP], BF16)
    Vs = dscr("Vs", [8, 128, TS // 128, 129], BF16); Vp = dscr("Vp", [8, 128, TP // 128, 129], BF16)
    HT = dscr("HT", [NCH, 128, 1024], BF16)
    ZG = dscr("ZG", [TL, D], BF16); ZD = dscr("ZD", [TL, D], BF16)
    OG = dscr("OG", [TL, D], BF16); OD = dscr("OD", [TL, D], BF16)
    OPART = dscr("OPART", [NCH, 128, 1024], F32)
    QBT = dscr("QBT", [NCH, 128, 512], BF16)
    UB = dscr("UB", [NCH, 128, 1024], F32)
    DECB = dscr("DECB", [NCH, 128, 4], F32)
    GATE = dscr("GATE", [128, 2048], F32)

    with ExitStack() as st0:
        S = Sync(nc, st0)

        def sb(stack, name, shape, dt):
            return Tl(stack.enter_context(nc.sbuf_tensor("sb_" + name, list(shape), dt)))

        ident = sb(st0, "ident", [128, 128], F32)
        identb = sb(st0, "identb", [128, 128], BF16)
        Aco = sb(st0, "Aco", [128, 2, 8], F32)
        Bco = sb(st0, "Bco", [128, 2, 8], F32)
        nlam = sb(st0, "nlam", [128, 1], F32)
        gd = sb(st0, "gd", [128, 128], F32)
        Sb_in = sb(st0, "Sb_in", [128, 1024], F32)
        Sf = sb(st0, "Sf", [128, 1024], F32)
        S.dma("c_id", ident.t[:], ident_d, W=[ident])
        S.op("act", lambda e: e.copy(identb.t[:], ident.t[:]), R=[ident], W=[identb])

        with ExitStack() as st:
            ps = Banks(nc, st, "ps0", 6, F32)
            c_sb = sb(st, "c_sb", [128, 8, 2], F32)
            gate_bc = sb(st, "gate_bc", [128, 2, 1024], F32)
            sc = sb(st, "sc", [128, 8, 2], F32)
            sc_bc = sb(st, "sc_bc", [128, 2, 8, 128], F32)
            wa = [sb(st, "wa%d" % i, [128, 3072], F32) for i in range(2)]
            badT = sb(st, "badT", [128, 24], F32)
            ngs = sb(st, "ngs", [128, 8], F32)
            bg = sb(st, "bg", [128, 1024], F32)
            lq = sb(st, "lq", [128, 256], F32)
            pr = sb(st, "pr", [128, 128], F32)
            e2 = sb(st, "e2", [128, 2], F32)
            tmp8 = sb(st, "tmp8", [128, 8], F32)
            dg = sb(st, "dg", [128, 128], F32)
            S.dma("p0a", c_sb.t[:], cT, W=[c_sb])
            S.dma("p0b", badT.t[:], b_adaT, W=[badT])
            S.dma("p0c", ngs.t[:], ngT, W=[ngs])
            S.dma("p0d", bg.t[:], b_gate.partition_broadcast(128), W=[bg])
            S.dma("p0e", lq.t[:], lqk.partition_broadcast(128), W=[lq])
            S.dma("p0f", dg.t[:], diff_gain.partition_broadcast(128), W=[dg])
            S.op("act", lambda e: e.activation(sc.t[:], c_sb.t[:], AF.Silu), R=[c_sb], W=[sc])
            for s in range(2):
                S.op("dve", lambda e: e.tensor_copy(
                    sc_bc.t[:, s], sc.t[:, :, s:s + 1].to_broadcast([128, 8, 128])), R=[sc], W=[sc_bc])
            _, modp, modt = ps.next()
            gps = [ps.next() for _ in range(4)]
            for kc in range(8):
                w = wa[kc % 2]
                S.dma("wa%d" % (kc % 2), w.t[:], w_ada[kc * 128:(kc + 1) * 128, :], W=[w])

                def mm(e):
                    last = None
                    for j in range(16):
                        last = e.matmul(modp[:, j * 2:(j + 1) * 2], w.t[:, j * 128:(j + 1) * 128],
                                        sc.t[:, kc, :], start=(kc == 0 and j == 0), stop=(kc == 7),
                                        skip_group_check=True)
                    return last
                S.op("pe", mm, R=[w, sc], W=[modt])
                for s in range(2):
                    for hf in range(2):
                        _, gp, gt = gps[s * 2 + hf]
                        S.op("pe", lambda e: e.matmul(
                            gp, sc_bc.t[:, s, kc, :], w.t[:, 2048 + hf * 512:2048 + (hf + 1) * 512],
                            start=(kc == 0), stop=(kc == 7)), R=[w, sc_bc], W=[gt])
            modv = modp[:, 0:32].rearrange("p (j s) -> p j s", s=2)
            for s in range(2):
                S.op("dve", lambda e: e.tensor_tensor(Bco.t[:, s, :], modv[:, 0:8, s], badT.t[:, 0:8], op=ALU.add),
                     R=[modt, badT], W=[Bco])
                S.op("dve", lambda e: e.tensor_tensor(tmp8.t[:], modv[:, 8:16, s], badT.t[:, 8:16], op=ALU.add),
                     R=[modt, badT], W=[tmp8])
                S.op("dve", lambda e: e.scalar_tensor_tensor(
                    Aco.t[:, s, :], tmp8.t[:], 1.0, ngs.t[:], op0=ALU.add, op1=ALU.mult),
                    R=[tmp8, ngs], W=[Aco])
                for hf in range(2):
                    _, gp, gt = gps[s * 2 + hf]
                    S.op("dve", lambda e: e.tensor_tensor(
                        gate_bc.t[:, s, hf * 512:(hf + 1) * 512], gp, bg.t[:, hf * 512:(hf + 1) * 512], op=ALU.add),
                        R=[gt, bg], W=[gate_bc])
            S.dma("st_gate", GATE.t, gate_bc.t[:].rearrange("p s c -> p (s c)"), R=[gate_bc], W=[GATE])
            S.op("dve", lambda e: e.tensor_tensor(pr.t[:], lq.t[:, 0:128], lq.t[:, 128:256], op=ALU.mult), R=[lq], W=[pr])
            S.op("dve", lambda e: e.tensor_reduce(
                e2.t[:], pr.t[:].rearrange("p (z d) -> p z d", z=2), axis=AX.X, op=ALU.add), R=[pr], W=[e2])
            S.op("act", lambda e: e.activation(e2.t[:], e2.t[:], AF.Exp), R=[e2], W=[e2])
            S.op("dve", lambda e: e.tensor_tensor(nlam.t[:], e2.t[:, 1:2], e2.t[:, 0:1], op=ALU.subtract), R=[e2], W=[nlam])
            S.op("dve", lambda e: e.tensor_scalar(nlam.t[:], nlam.t[:], -LAM_INIT, None, op0=ALU.add), R=[nlam], W=[nlam])
            S.op("dve", lambda e: e.tensor_scalar(gd.t[:], dg.t[:], 1.0 - LAM_INIT, None, op0=ALU.mult), R=[dg], W=[gd])
            S.barrier()

        with ExitStack() as st:
            W = sb(st, "Wres", [128, 8, WCOLS], BF16)
            with ExitStack() as stw:
                stg = [sb(stw, "wstg%d" % i, [128, 2400], F32) for i in range(2)]
                i = 0
                for kc in range(8):
                    for c0 in range(0, WCOLS, 2400):
                        g = stg[i % 2]
                        S.dma("wstg%d" % (i % 2), g.t[:], w_in[kc * 128:(kc + 1) * 128, c0:c0 + 2400], W=[g])
                        eng = ["dve", "pool", "act"][i % 3]
                        if eng == "act":
                            S.op("act", lambda e: e.copy(W.t[:, kc, c0:c0 + 2400], g.t[:]), R=[g], W=[W])
                        else:
                            S.op(eng, lambda e: e.tensor_copy(W.t[:, kc, c0:c0 + 2400], g.t[:]), R=[g], W=[W])
                        i += 1
                S.barrier()
            psf = Banks(nc, st, "psf", 6, F32)
            psb = Banks(nc, st, "psb", 2, BF16)
            tri = sb(st, "tri", [128, 4, 128], F32)
            msk = sb(st, "msk", [128, 2, 128], F32)
            negs = sb(st, "negs", [128, 1], F32)
            wal = sb(st, "wal", [33, 1024], F32)
            seg = sb(st, "seg", [128, 512], F32)
            lowT = sb(st, "lowT", [33, 128], F32)
            S.dma("c_tri", tri.t[:], tri_d.rearrange("k p i -> p k i"), W=[tri])
            S.dma("c_msk", msk.t[:], msk_d.rearrange("k p i -> p k i"), W=[msk])
            S.dma("c_neg", negs.t[:], negs_d, W=[negs])
            S.dma("c_wal", wal.t[:], walpha, W=[wal])
            S.dma("c_seg", seg.t[:], segm.partition_broadcast(128), W=[seg])
            S.op("dve", lambda e: e.memset(lowT.t[32:33, :], 1.0), W=[lowT])
            xt = [sb(st, "xt%d" % i, [128, 1024], F32) for i in range(2)]
            cs = [sb(st, "cs%d" % i, [128, 64], F32) for i in range(2)]
            junk = sb(st, "junk", [128, 1024], BF16)
            st2 = sb(st, "st2", [128, 2], F32)
            xn = sb(st, "xn", [128, 1024], F32)
            hT = [sb(st, "hT%d" % i, [128, 8, 128], BF16) for i in range(2)]
            lsps = [sb(st, "lsp%d" % i, [128, 1024], F32) for i in range(2)]
            Ea = sb(st, "Ea", [128, 512], F32)
            Eb = sb(st, "Eb", [128, 512], F32)
            qfT = sb(st, "qfT", [128, 512], BF16); kfT = sb(st, "kfT", [128, 512], BF16)
            qbT = sb(st, "qbT", [128, 512], BF16); kbT = sb(st, "kbT", [128, 512], BF16)
            khf = sb(st, "khf", [128, 512], BF16); khb = sb(st, "khb", [128, 512], BF16)
            dec = sb(st, "dec", [128, 8], F32)
            vtm = sb(st, "vtm", [128, 1024], BF16)
            m1 = sb(st, "m1", [128, 512], F32); m2 = sb(st, "m2", [128, 512], F32)
            PT = sb(st, "PT", [128, 512], BF16)
            oev = sb(st, "oev", [128, 1024], F32)
            ubev = sb(st, "ubev", [128, 1024], F32)
            Sf_bf = sb(st, "Sf_bf", [128, 1024], BF16)
            Dp = sb(st, "Dp", [128, 4], F32)
            co4 = sb(st, "co4", [128, 8], F32)
            zst = [sb(st, "zst%d" % i, [128, 1024], BF16) for i in range(2)]
            rt = [sb(st, "rt%d" % i, [128, 8, 32], F32) for i in range(4)]
            rq = sb(st, "rq", [128, 1024], BF16)
            qkT = [sb(st, "qkT%d" % i, [128, 8, 128], BF16) for i in range(2)]
            vaug = sb(st, "vaug", [128, 8, 129], BF16)
            S.op("pool", lambda e: e.memset(vaug.t[:], 1.0), W=[vaug])
            S.op("pool", lambda e: e.memset(Sf.t[:], 0.0), W=[Sf])
            S.op("pool", lambda e: e.memset(Sb_in.t[:], 0.0), W=[Sb_in])
            S.op("pool", lambda e: e.memset(Sf_bf.t[:], 0.0), W=[Sf_bf])
            S.op("pool", lambda e: e.memset(Dp.t[:], 1.0), W=[Dp])

            def proj_tm(hslot, c0, n=512):
                b, pap, pt = psf.next()

                def f(e):
                    last = None
                    for kc in range(8):
                        last = e.matmul(pap[:, 0:n], hslot.t[:, kc, :], W.t[:, kc, c0:c0 + n],
                                        start=(kc == 0), stop=(kc == 7))
                    return last
                S.op("pe", f, R=[hslot, W], W=[pt])
                return pap, pt

            def proj_fm(hslot, c0, ngrp, m=128):
                b, pap, pt = psf.next()

                def f(e):
                    last = None
                    for g in range(ngrp):
                        for kc in range(8):
                            last = e.matmul(pap[0:m, g * 128:(g + 1) * 128], W.t[:, kc, c0 + g * m:c0 + (g + 1) * m],
                                            hslot.t[:, kc, :], start=(kc == 0), stop=(kc == 7))
                    return last
                S.op("pe", f, R=[hslot, W], W=[pt])
                return pap, pt

            def rope_store(hslot, c0, cst, dstT, t0, slot):
                for g in range(2):
                    pap, pt = proj_tm(hslot, c0 + g * 512)
                    xv = pap.rearrange("p (b d) -> p b d", d=64)
                    cosb = cst.t[:, 0:32].unsqueeze(1).to_broadcast([128, 8, 32])
                    sinb = cst.t[:, 32:64].unsqueeze(1).to_broadcast([128, 8, 32])
                    S.op("dve", lambda e: e.tensor_tensor(rt[0].t[:], xv[:, :, 0:32], cosb, op=ALU.mult), R=[pt, cst], W=[rt[0]])
                    S.op("dve", lambda e: e.tensor_tensor(rt[1].t[:], xv[:, :, 32:64], sinb, op=ALU.mult), R=[pt, cst], W=[rt[1]])
                    S.op("dve", lambda e: e.tensor_tensor(rt[2].t[:], xv[:, :, 0:32], sinb, op=ALU.mult), R=[pt, cst], W=[rt[2]])
                    S.op("dve", lambda e: e.tensor_tensor(rt[3].t[:], xv[:, :, 32:64], cosb, op=ALU.mult), R=[pt, cst], W=[rt[3]])
                    rv = rq.t[:, g * 512:(g + 1) * 512].rearrange("p (b d) -> p b d", d=64)
                    S.op("pool", lambda e: e.tensor_tensor(rv[:, :, 0:32], rt[0].t[:], rt[1].t[:], op=ALU.subtract),
                         R=[rt[0], rt[1]], W=[rq])
                    S.op("pool", lambda e: e.tensor_tensor(rv[:, :, 32:64], rt[2].t[:], rt[3].t[:], op=ALU.add),
                         R=[rt[2], rt[3]], W=[rq])
                    yield
                b, bap, bt = psb.next()

                def tr(e):
                    last = None
                    for h in range(8):
                        last = e.transpose(bap[:, h * 128:(h + 1) * 128], rq.t[:, h * 128:(h + 1) * 128], identb.t[:])
                    return last
                S.op("pe", tr, R=[rq, identb], W=[bt])
                dst = qkT[slot]
                S.op("act", lambda e: e.copy(dst.t[:].rearrange("p h t -> p (h t)"), bap), R=[bt], W=[dst])
                S.dma("st_qk%d" % slot, dstT.t.rearrange("h p t -> p h t")[:, :, t0:t0 + 128], dst.t[:], R=[dst], W=[dstT])
                yield

            def front_a(td):
                it, xrows, ropesrc, s, mode, lt0, kv_dst, pre = td
                sl = it % 2
                x = xt[sl]; c_ = cs[sl]
                S.dma("ldx%d" % sl, x.t[:], xrows, W=[x])
                S.dma("ldc%d" % sl, c_.t[:], ropesrc, W=[c_])
                S.op("act", lambda e: e.activation(junk.t[:], x.t[:], AF.Square, accum_out=st2.t[:, 0:1]), R=[x], W=[junk, st2])
                S.op("act", lambda e: e.activation(st2.t[:, 1:2], st2.t[:, 0:1], AF.Ln, scale=1.0 / D, bias=EPS), R=[st2], W=[st2])
                S.op("act", lambda e: e.activation(st2.t[:, 1:2], st2.t[:, 1:2], AF.Exp, scale=-0.5), R=[st2], W=[st2])
                S.op("dve", lambda e: e.tensor_scalar(xn.t[:], x.t[:], st2.t[:, 1:2], None, op0=ALU.mult), R=[x, st2], W=[xn])

            def front_b(td):
                it, xrows, ropesrc, s, mode, lt0, kv_dst, pre = td
                sl = it % 2
                h = hT[sl]; lsp = lsps[sl]
                for g in range(2):
                    _, pap, pt = psf.next()

                    def tr(e):
                        last = None
                        for k4 in range(4):
                            kc = g * 4 + k4
                            last = e.transpose(pap[:, k4 * 128:(k4 + 1) * 128], xn.t[:, kc * 128:(kc + 1) * 128], ident.t[:])
                        return last
                    S.op("pe", tr, R=[xn, ident], W=[pt])
                    for k4 in range(4):
                        kc = g * 4 + k4
                        if k4 % 2 == 0:
                            S.op("act", lambda e: e.activation(
                                h.t[:, kc, :], pap[:, k4 * 128:(k4 + 1) * 128], AF.Identity,
                                scale=Aco.t[:, s, kc:kc + 1], bias=Bco.t[:, s, kc:kc + 1]), R=[pt, Aco, Bco], W=[h])
                        else:
                            S.op("dve", lambda e: e.tensor_scalar(
                                h.t[:, kc, :], pap[:, k4 * 128:(k4 + 1) * 128], Aco.t[:, s, kc:kc + 1],
                                Bco.t[:, s, kc:kc + 1], op0=ALU.mult, op1=ALU.add), R=[pt, Aco, Bco], W=[h])
                    yield
                if mode == "full":
                    S.dma("st_ht", HT.t[lt0 // 128], h.t[:].rearrange("p k t -> p (k t)"), R=[h], W=[HT])
                lp, lpt = proj_fm(h, C_LOW, 1, m=32)
                S.op("act", lambda e: e.copy(lowT.t[0:32, :], lp[0:32, 0:128]), R=[lpt], W=[lowT])
                yield
                for z in range(2):
                    _, pap, pt = psf.next()
                    S.op("pe", lambda e: e.matmul(pap, lowT.t[:], wal.t[:, z * 512:(z + 1) * 512], start=True, stop=True),
                         R=[lowT, wal], W=[pt])
                    S.op("act", lambda e: e.activation(lsp.t[:, z * 512:(z + 1) * 512], pap, AF.Exp, scale=-1.0), R=[pt], W=[lsp])
                S.op("act", lambda e: e.activation(lsp.t[:], lsp.t[:], AF.Ln, bias=1.0), R=[lsp], W=[lsp])
                yield

            def chain(td):
                it, xrows, ropesrc, s, mode, lt0, kv_dst, pre = td
                sl = it % 2
                h = hT[sl]; lsp = lsps[sl]
                ch = lt0 // 128 if mode == "full" else None
                if pre is not None:
                    pre()
                if mode == "full":
                    aqp, aqt = proj_fm(h, C_AQ, 4)
                    akp, akt = proj_fm(h, C_AK, 4)
                    for dr in range(2):
                        _, cp, ct = psf.next()

                        def cm(e):
                            last = None
                            for hh in range(4):
                                last = e.matmul(cp[:, hh * 128:(hh + 1) * 128],
                                                lsp.t[:, dr * 512 + hh * 128:dr * 512 + (hh + 1) * 128],
                                                tri.t[:, dr, :], start=True, stop=True)
                            return last
                        S.op("pe", cm, R=[lsp, tri], W=[ct])
                        S.op("act", lambda e: e.activation(Ea.t[:], cp, AF.Exp), R=[ct], W=[Ea])
                        S.op("act", lambda e: e.activation(Eb.t[:], cp, AF.Exp, scale=-1.0), R=[ct], W=[Eb])
                        qd, kd = (qfT, kfT) if dr == 0 else (qbT, kbT)
                        S.op("dve", lambda e: e.tensor_tensor(qd.t[:], aqp, Ea.t[:], op=ALU.mult), R=[aqt, Ea], W=[qd])
                        S.op("dve", lambda e: e.tensor_tensor(kd.t[:], akp, Eb.t[:], op=ALU.mult), R=[akt, Eb], W=[kd])
                        yield
                ktp, ktt = proj_tm(h, C_AK)
                for dr in range(2):
                    _, cp, ct = psf.next()
                    S.op("pe", lambda e: e.matmul(cp, tri.t[:, 2 + dr, :], lsp.t[:, dr * 512:(dr + 1) * 512], start=True, stop=True),
                         R=[lsp, tri], W=[ct])
                    Ex = Ea if dr == 0 else Eb
                    S.op("act", lambda e: e.activation(Ex.t[:], cp, AF.Exp), R=[ct], W=[Ex])
                    kh = khf if dr == 0 else khb
                    S.op("dve", lambda e: e.tensor_tensor(kh.t[:], ktp, Ex.t[:], op=ALU.mult), R=[ktt, Ex], W=[kh])
                    yield
                _, tp_, tt_ = psf.next()

                def tot(e):
                    last = None
                    for j in range(8):
                        last = e.matmul(tp_[:, j:j + 1], lsp.t[:, j * 128:(j + 1) * 128], negs.t[:], start=True, stop=True)
                    return last
                S.op("pe", tot, R=[lsp, negs], W=[tt_])
                S.op("act", lambda e: e.activation(dec.t[:], tp_[:, 0:8], AF.Exp), R=[tt_], W=[dec])
                for g in range(2):
                    pap, pt = proj_tm(h, C_AV + g * 512)
                    S.op("act", lambda e: e.copy(vtm.t[:, g * 512:(g + 1) * 512], pap), R=[pt], W=[vtm])
                yield
                if mode == "full":
                    sp_ = []
                    for dr in range(2):
                        _, pap, pt = psf.next()
                        qd, kd = (qfT, kfT) if dr == 0 else (qbT, kbT)

                        def scm(e):
                            last = None
                            for hh in range(4):
                                last = e.matmul(pap[:, hh * 128:(hh + 1) * 128], kd.t[:, hh * 128:(hh + 1) * 128],
                                                qd.t[:, hh * 128:(hh + 1) * 128], start=True, stop=True)
                            return last
                        S.op("pe", scm, R=[qd, kd], W=[pt])
                        sp_.append((pap, pt))
                    for dr in range(2):
                        pap, pt = sp_[dr]
                        mm_ = m1 if dr == 0 else m2
                        S.op("dve", lambda e: e.tensor_tensor(
                            mm_.t[:].rearrange("p (h i) -> p h i", h=4), pap.rearrange("p (h i) -> p h i", h=4),
                            msk.t[:, dr, :].unsqueeze(1).to_broadcast([128, 4, 128]), op=ALU.mult), R=[pt, msk], W=[mm_])
                    S.op("pool", lambda e: e.tensor_tensor(PT.t[:], m1.t[:], m2.t[:], op=ALU.add), R=[m1, m2], W=[PT])
                    yield
                    for g in range(2):
                        _, pap, pt = psf.next()

                        def om(e):
                            last = None
                            for h2 in range(2):
                                hh = g * 2 + h2
                                e.matmul(pap[:, h2 * 256:(h2 + 1) * 256], PT.t[:, hh * 128:(hh + 1) * 128],
                                         vtm.t[:, hh * 256:(hh + 1) * 256], start=True, stop=False)
                                last = e.matmul(pap[:, h2 * 256:(h2 + 1) * 256], qfT.t[:, hh * 128:(hh + 1) * 128],
                                                Sf_bf.t[:, hh * 256:(hh + 1) * 256], start=False, stop=True)
                            return last
                        S.op("pe", om, R=[PT, vtm, qfT, Sf_bf], W=[pt])
                        S.op("act", lambda e: e.activation(oev.t[:, g * 512:(g + 1) * 512], pap, AF.Copy, scale=QS), R=[pt], W=[oev])
                    S.dma("st_op", OPART.t[ch], oev.t[:], R=[oev], W=[OPART])
                    S.dma("st_qb", QBT.t[ch], qbT.t[:], R=[qbT], W=[QBT])
                    S.dma("st_db", DECB.t[ch], dec.t[:, 4:8], R=[dec], W=[DECB])
                    yield
                for dr in range(2):
                    kh = khf if dr == 0 else khb
                    ups = []
                    for g in range(2):
                        _, pap, pt = psf.next()

                        def um(e):
                            last = None
                            for h2 in range(2):
                                hh = g * 2 + h2
                                last = e.matmul(pap[:, h2 * 256:(h2 + 1) * 256], kh.t[:, hh * 128:(hh + 1) * 128],
                                                vtm.t[:, hh * 256:(hh + 1) * 256], start=True, stop=True)
                            return last
                        S.op("pe", um, R=[kh, vtm], W=[pt])
                        ups.append((pap, pt))
                    if mode == "full":
                        if dr == 0:
                            for hh in range(4):
                                pap, pt = ups[hh // 2]
                                S.op("dve", lambda e: e.scalar_tensor_tensor(
                                    Sf.t[:, hh * 256:(hh + 1) * 256], Sf.t[:, hh * 256:(hh + 1) * 256], dec.t[:, hh:hh + 1],
                                    pap[:, (hh % 2) * 256:(hh % 2 + 1) * 256], op0=ALU.mult, op1=ALU.add),
                                    R=[Sf, dec, pt], W=[Sf])
                            S.op("pool", lambda e: e.tensor_copy(Sf_bf.t[:], Sf.t[:]), R=[Sf], W=[Sf_bf])
                        else:
                            for g in range(2):
                                pap, pt = ups[g]
                                S.op("act", lambda e: e.copy(ubev.t[:, g * 512:(g + 1) * 512], pap), R=[pt], W=[ubev])
                            S.dma("st_ub", UB.t[ch], ubev.t[:], R=[ubev], W=[UB])
                    else:
                        n = it
                        if dr == 0:
                            S.op("dve", lambda e: e.scalar_tensor_tensor(
                                co4.t[:, 0:4], dec.t[:, 0:4], seg.t[:, n:n + 1],
                                seg.t[:, 128 + n:129 + n].to_broadcast([128, 4]), op0=ALU.mult, op1=ALU.add),
                                R=[dec, seg], W=[co4])
                            for g in range(2):
                                pap, pt = ups[g]
                                S.op("act", lambda e: e.activation(ubev.t[:, g * 512:(g + 1) * 512], pap, AF.Copy,
                                                                   scale=seg.t[:, n:n + 1]), R=[pt, seg], W=[ubev])
                            for hh in range(4):
                                S.op("dve", lambda e: e.scalar_tensor_tensor(
                                    Sf.t[:, hh * 256:(hh + 1) * 256], Sf.t[:, hh * 256:(hh + 1) * 256], co4.t[:, hh:hh + 1],
                                    ubev.t[:, hh * 256:(hh + 1) * 256], op0=ALU.mult, op1=ALU.add),
                                    R=[Sf, co4, ubev], W=[Sf])
                        else:
                            S.op("dve", lambda e: e.tensor_scalar(co4.t[:, 4:8], Dp.t[:], seg.t[:, 256 + n:257 + n], None, op0=ALU.mult),
                                 R=[Dp, seg], W=[co4])
                            for hh in range(4):
                                pap, pt = ups[hh // 2]
                                S.op("dve", lambda e: e.scalar_tensor_tensor(
                                    Sb_in.t[:, hh * 256:(hh + 1) * 256], pap[:, (hh % 2) * 256:(hh % 2 + 1) * 256],
                                    co4.t[:, 4 + hh:5 + hh], Sb_in.t[:, hh * 256:(hh + 1) * 256], op0=ALU.mult, op1=ALU.add),
                                    R=[Sb_in, co4, pt], W=[Sb_in])
                            S.op("dve", lambda e: e.scalar_tensor_tensor(
                                co4.t[:, 4:8], dec.t[:, 4:8], seg.t[:, 256 + n:257 + n],
                                seg.t[:, 384 + n:385 + n].to_broadcast([128, 4]), op0=ALU.mult, op1=ALU.add),
                                R=[dec, seg], W=[co4])
                            S.op("dve", lambda e: e.tensor_tensor(Dp.t[:], Dp.t[:], co4.t[:, 4:8], op=ALU.mult), R=[Dp, co4], W=[Dp])
                    yield

            def bulk(td):
                it, xrows, ropesrc, s, mode, lt0, kv_dst, pre = td
                sl = it % 2
                h = hT[sl]; c_ = cs[sl]
                if kv_dst is not None:
                    KTd, Vd, t0 = kv_dst
                    yield from rope_store(h, C_DK, c_, KTd, t0, 1)
                    for g in range(2):
                        pap, pt = proj_tm(h, C_DV + g * 512)
                        S.op("act", lambda e: e.copy(vaug.t[:, g * 4:(g + 1) * 4, 0:128],
                                                     pap.rearrange("p (h e) -> p h e", e=128)), R=[pt], W=[vaug])
                    S.dma("st_v", Vd.t[:, :, t0 // 128, :].rearrange("h p e -> p h e"), vaug.t[:], R=[vaug], W=[Vd])
                    yield
                if mode == "full":
                    yield from rope_store(h, C_DQ, c_, QT, lt0, 0)
                    for zi, (c0, dstz) in enumerate(((C_AZ, ZG), (C_DZ, ZD))):
                        zt = zst[zi]
                        for g in range(2):
                            pap, pt = proj_tm(h, c0 + g * 512)
                            S.op("act", lambda e: e.activation(zt.t[:, g * 512:(g + 1) * 512], pap, AF.Silu), R=[pt], W=[zt])
                        S.dma("st_z%d" % zi, dstz.t[lt0:lt0 + 128, :], zt.t[:], R=[zt], W=[dstz])
                        yield

            def interleave(gens):
                gens = list(gens)
                while gens:
                    for g in list(gens):
                        try:
                            next(g)
                        except StopIteration:
                            gens.remove(g)

            def pre_own():
                S.op("pool", lambda e: e.tensor_copy(Sf_bf.t[:], Sf.t[:]), R=[Sf], W=[Sf_bf])

            def pre_sample():
                S.op("pool", lambda e: e.memset(Sf.t[:], 0.0), W=[Sf])
                S.op("pool", lambda e: e.memset(Sf_bf.t[:], 0.0), W=[Sf_bf])

            tds = []
            for n in range((TP - TO) // 128):
                tds.append((n, xr[n * 128:(n + 1) * 128, :], rope_r[n * 128:(n + 1) * 128, :], 1, "kv", None,
                            (KTp, Vp, TO + n * 128), None))
            for n in range(TO // 128):
                tds.append((n, xo[n * 128:(n + 1) * 128, :], rope_o[n * 128:(n + 1) * 128, :], 1, "full", TS + n * 128,
                            (KTp, Vp, n * 128), pre_own if n == 0 else None))
            for n in range(TS // 128):
                tds.append((n, xs[n * 128:(n + 1) * 128, :], rope_t[n * 128:(n + 1) * 128, :], 0, "full", n * 128, (KTs, Vs, n * 128),
                            pre_sample if n == 0 else None))
            front_a(tds[0])
            interleave([front_b(tds[0])])
            for i, td in enumerate(tds):
                gens = [chain(td), bulk(td)]
                if i + 1 < len(tds):
                    front_a(tds[i + 1])
                    gens.append(front_b(tds[i + 1]))
                interleave(gens)
            S.barrier()

        with ExitStack() as st:
            psa = Banks(nc, st, "psa", 8, F32)
            KTt = [sb(st, "KTt%d" % i, [128, TP], BF16) for i in range(2)]
            Vt = [sb(st, "Vt%d" % i, [128, TP // 128, 129], BF16) for i in range(2)]
            QTt = [sb(st, "QTt%d" % i, [128, TS], BF16) for i in range(2)]
            Et = [sb(st, "Et%d" % i, [128, 1024], BF16) for i in range(3)]
            accs = [sb(st, "accs0", [128, 8, 129], F32)] * 2
            rr8 = sb(st, "rr8", [128, 8], F32)
            rl4 = sb(st, "rl4", [128, 4], F32)
            t_o = sb(st, "t_o", [128, 128], F32)
            o_os = [sb(st, "o_o%d" % i, [128, 4, 128], F32) for i in range(2)]
            ss4s = [sb(st, "ss4c%d" % i, [128, 4], F32) for i in range(2)]
            pending = []
            jk2 = sb(st, "jk2", [128, 128], F32)
            odt = [sb(st, "odt%d" % i, [128, 4, 128], BF16) for i in range(2)]
            sc_t = [Tl(None), Tl(None)]
            acc_t = [Tl(None), Tl(None), Tl(None)]
            ggain = sb(st, "ggain", [128, 1024], F32)
            S.dma("c_gg", ggain.t[:], gla_gain.partition_broadcast(128), W=[ggain])
            Sb = sb(st, "Sb", [128, 1024], F32)
            Sb_bf = sb(st, "Sb_bf", [128, 1024], BF16)
            op_ = sb(st, "op_", [128, 1024], F32)
            qb_ = [sb(st, "qb%d" % i, [128, 512], BF16) for i in range(2)]
            ub_ = sb(st, "ub_", [128, 1024], F32)
            db_ = [sb(st, "db%d" % i, [128, 4], F32) for i in range(2)]
            zg_ = [sb(st, "zg0", [128, 1024], BF16)] * 2
            osum = sb(st, "osum", [128, 1024], F32)
            gz = sb(st, "gz", [128, 1024], F32)
            jk = sb(st, "jk", [128, 256], F32)
            ssg = sb(st, "ssg", [128, 4], F32)
            ogt = [sb(st, "ogt0", [128, 1024], BF16)] * 2
            b7 = Tl(None)

            def sweep2():
                it = 0
                for (c_lo, c_hi, init_from) in ((0, TS // 128, None), (TS // 128, NCH, Sb_in)):
                    if init_from is None:
                        S.op("pool", lambda e: e.memset(Sb.t[:], 0.0), W=[Sb])
                    else:
                        S.op("pool", lambda e: e.tensor_copy(Sb.t[:], init_from.t[:]), R=[init_from], W=[Sb])
                    S.op("pool", lambda e: e.tensor_copy(Sb_bf.t[:], Sb.t[:]), R=[Sb], W=[Sb_bf])
                    for ch in range(c_hi - 1, c_lo - 1, -1):
                        sl = it % 2
                        it += 1
                        q_b, d_b, z_g, og_t = qb_[sl], db_[sl], zg_[sl], ogt[sl]
                        S.dma("b_op", op_.t[:], OPART.t[ch], R=[OPART], W=[op_], q="pool")
                        S.dma("b_qb%d" % sl, q_b.t[:], QBT.t[ch], R=[QBT], W=[q_b], q="pool")
                        S.dma("b_ub", ub_.t[:], UB.t[ch], R=[UB], W=[ub_], q="pool")
                        S.dma("b_db%d" % sl, d_b.t[:], DECB.t[ch], R=[DECB], W=[d_b], q="pool")
                        S.dma("b_zg", z_g.t[:], ZG.t[ch * 128:(ch + 1) * 128, :], R=[ZG], W=[z_g], q="pool")
                        yield
                        S.op("pool", lambda e: e.tensor_tensor(gz.t[:], z_g.t[:], ggain.t[:], op=ALU.mult), R=[z_g, ggain], W=[gz])
                        for g in range(2):
                            pap = psa.ap(7)

                            def im(e):
                                last = None
                                for h2 in range(2):
                                    hh = g * 2 + h2
                                    last = e.matmul(pap[:, h2 * 256:(h2 + 1) * 256], q_b.t[:, hh * 128:(hh + 1) * 128],
                                                    Sb_bf.t[:, hh * 256:(hh + 1) * 256], start=True, stop=True)
                                return last
                            S.op("pe", im, R=[q_b, Sb_bf], W=[b7])
                            S.op("dve", lambda e: e.scalar_tensor_tensor(
                                osum.t[:, g * 512:(g + 1) * 512], pap, QS, op_.t[:, g * 512:(g + 1) * 512],
                                op0=ALU.mult, op1=ALU.add), R=[b7, op_], W=[osum])
                            yield
                        for hh in range(4):
                            S.op("dve", lambda e: e.scalar_tensor_tensor(
                                Sb.t[:, hh * 256:(hh + 1) * 256], Sb.t[:, hh * 256:(hh + 1) * 256], d_b.t[:, hh:hh + 1],
                                ub_.t[:, hh * 256:(hh + 1) * 256], op0=ALU.mult, op1=ALU.add), R=[Sb, d_b, ub_], W=[Sb])
                        S.op("pool", lambda e: e.tensor_copy(Sb_bf.t[:], Sb.t[:]), R=[Sb], W=[Sb_bf])
                        yield
                        for hh in range(4):
                            S.op("dve", lambda e: e.tensor_tensor(jk.t[:], osum.t[:, hh * 256:(hh + 1) * 256],
                                                                  osum.t[:, hh * 256:(hh + 1) * 256], op=ALU.mult), R=[osum], W=[jk])
                            S.op("dve", lambda e: e.reduce_sum(ssg.t[:, hh:hh + 1], jk.t[:], axis=AX.X), R=[jk], W=[ssg])
                        yield
                        S.op("act", lambda e: e.activation(ssg.t[:], ssg.t[:], AF.Ln, scale=1.0 / 256, bias=EPS), R=[ssg], W=[ssg])
                        S.op("act", lambda e: e.activation(ssg.t[:], ssg.t[:], AF.Exp, scale=-0.5), R=[ssg], W=[ssg])
                        yield
                        for hh in range(4):
                            S.op("dve", lambda e: e.scalar_tensor_tensor(
                                og_t.t[:, hh * 256:(hh + 1) * 256], osum.t[:, hh * 256:(hh + 1) * 256], ssg.t[:, hh:hh + 1],
                                gz.t[:, hh * 256:(hh + 1) * 256], op0=ALU.mult, op1=ALU.mult), R=[osum, ssg, gz], W=[og_t])
                        S.dma("b_og", OG.t[ch * 128:(ch + 1) * 128, :], og_t.t[:], R=[og_t], W=[OG], q="pool")
                        yield

            sw2 = sweep2()
            steps = []
            heads = []
            for ji, (Tq, Tk, q0, KTd, Vd) in enumerate(((TS, TS, 0, KTs, Vs), (TO, TP, TS, KTp, Vp))):
                for h in range(8):
                    heads.append((Tq, Tk, q0, KTd, Vd, h))
            for hidx, (Tq, Tk, q0, KTd, Vd, h) in enumerate(heads):
                nb = Tk // 128
                for qt in range(Tq // 512):
                    for kb in range(nb):
                        steps.append((hidx, qt, kb, nb))
            loaded = set()

            def load_head(hidx):
                if hidx in loaded or hidx >= len(heads):
                    return
                loaded.add(hidx)
                Tq, Tk, q0, KTd, Vd, h = heads[hidx]
                sl = hidx % 2
                S.dma("c_k%d" % sl, KTt[sl].t[:, 0:Tk], KTd.t[h], R=[KTd], W=[KTt[sl]])
                S.dma("c_v%d" % sl, Vt[sl].t[:, 0:Tk // 128, :], Vd.t[h], R=[Vd], W=[Vt[sl]])
                S.dma("c_q%d" % sl, QTt[sl].t[:, 0:Tq], QT.t[h, :, q0:q0 + Tq], R=[QT], W=[QTt[sl]])

            def qk(si):
                hidx, qt, kb, nb = steps[si]
                load_head(hidx)
                sl = hidx % 2
                kt, qt_ = KTt[sl], QTt[sl]
                sb0 = (si % 2) * 2

                def f(e):
                    last = None
                    for z in range(2):
                        last = e.matmul(psa.ap(sb0 + z), kt.t[z * 64:(z + 1) * 64, kb * 128:(kb + 1) * 128],
                                        qt_.t[z * 64:(z + 1) * 64, qt * 512:(qt + 1) * 512],
                                        start=True, stop=True, tile_position=(z * 64, 0))
                    return last
                S.op("pe", f, R=[kt, qt_], W=[sc_t[si % 2]])
                E = Et[si % 3]
                S.op("act", lambda e: e.activation(E.t[:], psa.t[:, sb0 * 512:(sb0 + 2) * 512], AF.Exp, scale=0.125),
                     R=[sc_t[si % 2]], W=[E])

            def pv(si):
                hidx, qt, kb, nb = steps[si]
                vt = Vt[hidx % 2]
                E = Et[si % 3]

                for bk in range(3):
                    def f(e):
                        last = None
                        for a in range(bk * 3, min(bk * 3 + 3, 8)):
                            z, qb = a // 4, a % 4
                            off = (4 + a // 3) * 512 + (a % 3) * 129
                            last = e.matmul(psa.t[:, off:off + 129], E.t[:, z * 512 + qb * 128:z * 512 + (qb + 1) * 128],
                                            vt.t[:, kb, :], start=(kb == 0 and a % 3 == 0), stop=(kb == nb - 1),
                                            skip_group_check=True)
                        return last
                    S.op("pe", f, R=[E, vt], W=[acc_t[bk]])

            epi_i = [0]

            def epilogue(si):
                hidx, qt, kb, nb = steps[si]
                Tq, Tk, q0, KTd, Vd, h = heads[hidx]
                ei_ = epi_i[0]
                ac = accs[ei_ % 2]
                od_t = odt[ei_ % 2]
                o_o, ss4 = o_os[ei_ % 2], ss4s[ei_ % 2]
                epi_i[0] += 1
                for bk in range(3):
                    na = 3 if bk < 2 else 2
                    S.op("dve", lambda e: e.tensor_copy(
                        ac.t[:, bk * 3:bk * 3 + na, :],
                        psa.t[:, (4 + bk) * 512:(4 + bk) * 512 + na * 129].rearrange("p (a c) -> p a c", c=129)),
                        R=[acc_t[bk]], W=[ac])
                S.op("dve", lambda e: e.reciprocal(rr8.t[:], ac.t[:, :, 128]), R=[ac], W=[rr8])
                S.op("dve", lambda e: e.tensor_scalar(rl4.t[:], rr8.t[:, 4:8], nlam.t[:, 0:1], None, op0=ALU.mult), R=[rr8, nlam], W=[rl4])
                for qb in range(4):
                    S.op("dve", lambda e: e.tensor_scalar(t_o.t[:], ac.t[:, qb, 0:128], rr8.t[:, qb:qb + 1], None, op0=ALU.mult),
                         R=[ac, rr8], W=[t_o])
                    S.op("dve", lambda e: e.scalar_tensor_tensor(
                        o_o.t[:, qb, :], ac.t[:, 4 + qb, 0:128], rl4.t[:, qb:qb + 1], t_o.t[:], op0=ALU.mult, op1=ALU.add),
                        R=[ac, rl4, t_o], W=[o_o])
                    S.op("dve", lambda e: e.tensor_tensor(jk2.t[:], o_o.t[:, qb, :], o_o.t[:, qb, :], op=ALU.mult), R=[o_o], W=[jk2])
                    S.op("dve", lambda e: e.reduce_sum(ss4.t[:, qb:qb + 1], jk2.t[:], axis=AX.X), R=[jk2], W=[ss4])
                pending.append((si + 6, lambda: epilogue2(si, od_t, ei_)))

            def epilogue2(si, od_t, ei_):
                hidx, qt, kb, nb = steps[si]
                Tq, Tk, q0, KTd, Vd, h = heads[hidx]
                o_o, ss4 = o_os[ei_ % 2], ss4s[ei_ % 2]
                S.op("act", lambda e: e.activation(ss4.t[:], ss4.t[:], AF.Ln, scale=1.0 / 128, bias=EPS), R=[ss4], W=[ss4])
                S.op("act", lambda e: e.activation(ss4.t[:], ss4.t[:], AF.Exp, scale=-0.5), R=[ss4], W=[ss4])
                for qb in range(4):
                    S.op("dve", lambda e: e.scalar_tensor_tensor(
                        od_t.t[:, qb, :], o_o.t[:, qb, :], ss4.t[:, qb:qb + 1], gd.t[:], op0=ALU.mult, op1=ALU.mult),
                        R=[o_o, ss4, gd], W=[od_t])
                r0 = q0 + qt * 512
                S.dma("c_od%d" % (ei_ % 2),
                      OD.t[r0:r0 + 512, h * 128:(h + 1) * 128].rearrange("(qb p) e -> p qb e", p=128), od_t.t[:], R=[od_t], W=[OD])

            load_head(0)
            load_head(1)
            ns = len(steps)
            qk(0)
            qk(1)
            sw_alive = [True]

            def sw_step():
                if sw_alive[0]:
                    try:
                        next(sw2)
                    except StopIteration:
                        sw_alive[0] = False

            for si in range(ns):
                if si + 2 < ns:
                    qk(si + 2)
                pv(si)
                while pending and pending[0][0] <= si:
                    pending.pop(0)[1]()
                if si % 8 == 4 and steps[si][2] + 8 < steps[si][3]:
                    sw_step()
                hidx, qt, kb, nb = steps[si]
                if kb == nb - 1:
                    epilogue(si)
                    if qt == heads[hidx][0] // 512 - 1:
                        load_head(hidx + 2)
            while pending:
                pending.pop(0)[1]()
            while sw_alive[0]:
                sw_step()
            S.barrier()

        with ExitStack() as st:
            psf = Banks(nc, st, "psf3", 6, F32)
            psb = Banks(nc, st, "psb3", 2, BF16)
            Wt = {}
            wspec = (("g", w_bog, 0, 1024), ("d", w_bod, 0, 1024), ("o", w_outd, 0, 1024), ("m", w_in, C_MG, 2048))
            for nm, src, c0, ncol in wspec:
                Wt[nm] = sb(st, "W_" + nm, [128, 8, ncol], BF16)
            with ExitStack() as stw:
                stg = [sb(stw, "wstg2%d" % i, [128, 2048], F32) for i in range(2)]
                i = 0
                for nm, src, c0, ncol in wspec:
                    for kc in range(8):
                        g = stg[i % 2]
                        S.dma("wstg2%d" % (i % 2), g.t[:, 0:ncol], src[kc * 128:(kc + 1) * 128, c0:c0 + ncol], W=[g])
                        eng = ["dve", "pool"][i % 2]
                        S.op(eng, lambda e: e.tensor_copy(Wt[nm].t[:, kc, :], g.t[:, 0:ncol]), R=[g], W=[Wt[nm]])
                        i += 1
                S.barrier()
            fg = sb(st, "fg", [128, 1024], F32)
            S.dma("c_fg", fg.t[:], fgain.partition_broadcast(128), W=[fg])
            gate_bc = sb(st, "gate_bc2", [128, 2, 1024], F32)
            S.dma("c_gate", gate_bc.t[:].rearrange("p s c -> p (s c)"), GATE.t, R=[GATE], W=[gate_bc])
            xt = [sb(st, "xd%d" % i, [128, 1024], F32) for i in range(3)]
            hTt = [sb(st, "hd%d" % i, [128, 8, 128], BF16) for i in range(3)]
            ogl = [sb(st, "ogl%d" % i, [128, 1024], BF16) for i in range(3)]
            odl = [sb(st, "odl%d" % i, [128, 1024], BF16) for i in range(3)]
            zdl = [sb(st, "zdl%d" % i, [128, 1024], BF16) for i in range(3)]
            odg = sb(st, "odg", [128, 1024], BF16)
            oT = [sb(st, "oT%d" % i, [128, 8, 128], BF16) for i in range(2)]
            gm = sb(st, "gm", [128, 2048], F32)
            ta = sb(st, "ta", [128, 1024], F32)
            tb_ = sb(st, "tb", [128, 1024], F32)
            mg = sb(st, "mg", [128, 1024], BF16)
            mT = sb(st, "mT", [128, 8, 128], BF16)
            r_ = sb(st, "r_", [128, 1024], F32)
            jk3 = sb(st, "jk3", [128, 1024], BF16)
            s3 = sb(st, "s3", [128, 2], F32)
            yt = [sb(st, "yt%d" % i, [128, 1024], F32) for i in range(2)]

            def transpose8(src, dst):
                b, bap, bt = psb.next()

                def tr(e):
                    last = None
                    for k in range(8):
                        last = e.transpose(bap[:, k * 128:(k + 1) * 128], src.t[:, k * 128:(k + 1) * 128], identb.t[:])
                    return last
                S.op("pe", tr, R=[src, identb], W=[bt])
                S.op("act", lambda e: e.copy(dst.t[:].rearrange("p k t -> p (k t)"), bap), R=[bt], W=[dst])

            def mm_tm(lhs, wname, c0):
                b, pap, pt = psf.next()

                def f(e):
                    last = None
                    for kc in range(8):
                        last = e.matmul(pap, lhs.t[:, kc, :], Wt[wname].t[:, kc, c0:c0 + 512], start=(kc == 0), stop=(kc == 7))
                    return last
                S.op("pe", f, R=[lhs, Wt[wname]], W=[pt])
                return pap, pt

            mgs = [mg, sb(st, "mg1", [128, 1024], BF16)]
            tc = sb(st, "tc", [128, 1024], F32)

            def srcdst(ch):
                s = 0 if ch < TS // 128 else 1
                if s == 0:
                    return s, xs[ch * 128:(ch + 1) * 128, :], y_s[ch * 128:(ch + 1) * 128, :]
                c2 = ch - TS // 128
                return s, xo[c2 * 128:(c2 + 1) * 128, :], y_o[c2 * 128:(c2 + 1) * 128, :]

            def L(ch):
                sl = ch % 3
                s, xsrc, ydst = srcdst(ch)
                x, hh_, og_, od_, zd_ = xt[sl], hTt[sl], ogl[sl], odl[sl], zdl[sl]
                S.dma("d_x%d" % sl, x.t[:], xsrc, W=[x])
                S.dma("d_h%d" % sl, hh_.t[:].rearrange("p k t -> p (k t)"), HT.t[ch], R=[HT], W=[hh_])
                S.dma("d_og%d" % sl, og_.t[:], OG.t[ch * 128:(ch + 1) * 128, :], R=[OG], W=[og_])
                S.dma("d_od%d" % sl, od_.t[:], OD.t[ch * 128:(ch + 1) * 128, :], R=[OD], W=[od_])
                S.dma("d_zd%d" % sl, zd_.t[:], ZD.t[ch * 128:(ch + 1) * 128, :], R=[ZD], W=[zd_])

            def A1(ch):
                sl = ch % 3
                x, hh_, og_, od_, zd_ = xt[sl], hTt[sl], ogl[sl], odl[sl], zdl[sl]
                S.op("pool", lambda e: e.tensor_tensor(odg.t[:], od_.t[:], zd_.t[:], op=ALU.mult), R=[od_, zd_], W=[odg])
                transpose8(og_, oT[0])
                for g in range(4):
                    pap, pt = mm_tm(hh_, "m", g * 512)
                    S.op("act", lambda e: e.activation(gm.t[:, g * 512:(g + 1) * 512], pap, AF.Sigmoid), R=[pt], W=[gm])
                transpose8(odg, oT[1])

            def A2a(ch):
                for g in range(2):
                    pg, ptg = mm_tm(oT[0], "g", g * 512)
                    S.op("dve", lambda e: e.tensor_tensor(ta.t[:, g * 512:(g + 1) * 512], pg, gm.t[:, g * 512:(g + 1) * 512], op=ALU.mult),
                         R=[ptg, gm], W=[ta])

            def A2b(ch):
                for g in range(2):
                    pd, ptd = mm_tm(oT[1], "d", g * 512)
                    S.op("dve", lambda e: e.tensor_tensor(tb_.t[:, g * 512:(g + 1) * 512], pd, gm.t[:, 1024 + g * 512:1024 + (g + 1) * 512], op=ALU.mult),
                         R=[ptd, gm], W=[tb_])
                m_ = mgs[ch % 2]
                S.op("dve", lambda e: e.tensor_tensor(m_.t[:], ta.t[:], tb_.t[:], op=ALU.add), R=[ta, tb_], W=[m_])

            def B1(ch):
                transpose8(mgs[ch % 2], mT)

            def B2(ch):
                s, xsrc, ydst = srcdst(ch)
                for g in range(2):
                    po, pto = mm_tm(mT, "o", g * 512)
                    S.op("dve", lambda e: e.tensor_tensor(tc.t[:, g * 512:(g + 1) * 512], po, gate_bc.t[:, s, g * 512:(g + 1) * 512], op=ALU.mult),
                         R=[pto, gate_bc], W=[tc])

            def B3(ch):
                sl = ch % 2
                s, xsrc, ydst = srcdst(ch)
                x = xt[ch % 3]
                S.op("dve", lambda e: e.tensor_tensor(r_.t[:], tc.t[:], x.t[:], op=ALU.add), R=[tc, x], W=[r_])
                S.op("act", lambda e: e.activation(jk3.t[:], r_.t[:], AF.Square, accum_out=s3.t[:, 0:1]), R=[r_], W=[jk3, s3])
                S.op("act", lambda e: e.activation(s3.t[:, 1:2], s3.t[:, 0:1], AF.Ln, scale=1.0 / D, bias=EPS), R=[s3], W=[s3])
                S.op("act", lambda e: e.activation(s3.t[:, 1:2], s3.t[:, 1:2], AF.Exp, scale=-0.5), R=[s3], W=[s3])
                y = yt[sl]
                S.op("dve", lambda e: e.scalar_tensor_tensor(y.t[:], r_.t[:], s3.t[:, 1:2], fg.t[:], op0=ALU.mult, op1=ALU.mult),
                     R=[r_, s3, fg], W=[y])
                S.dma("d_y%d" % sl, ydst, y.t[:], R=[y])

            L(0)
            L(1)
            A1(0)
            A2a(0)
            A2b(0)
            for ch in range(NCH):
                nxt = ch + 1 < NCH
                if ch + 2 < NCH:
                    L(ch + 2)
                if nxt:
                    A1(ch + 1)
                B1(ch)
                if nxt:
                    A2a(ch + 1)
                B2(ch)
                if nxt:
                    A2b(ch + 1)
                B3(ch)
            S.barrier()
    return nc


_CACHE = {}


def _consts():
    if "c" in _CACHE:
        return _CACHE["c"]
    i = np.arange(128)
    t_, i_ = i[:, None], i[None, :]
    neg = np.float32(-1.0 / 16.0)
    tri = np.stack([
        (t_ <= i_), (t_ >= i_), (t_ > i_), (t_ < i_)]).astype(np.float32) * neg
    msk = np.stack([(t_ <= i_), (t_ >= i_)]).astype(np.float32)
    negs = np.full((128, 1), neg, np.float32)
    half = 32
    inv = (1.0 / (np.float32(10000.0) ** (np.arange(half, dtype=np.float32) / np.float32(half)))).astype(np.float32)
    ang = (np.arange(TP, dtype=np.float32)[:, None] * inv[None, :]).astype(np.float32)
    rope = np.concatenate([np.cos(ang), np.sin(ang)], axis=1).astype(np.float32)
    _CACHE["c"] = dict(tri=tri, msk=msk, negs=negs, rope=rope, ident=np.eye(128, dtype=np.float32))
    return _CACHE["c"]


def kernel(x_prompt, x_sample, c_prompt, c_sample, w_ada, b_ada, norm_gain, w_in,
           w_alpha, b_alpha, gla_norm_gain, lambda_q, lambda_k, diff_norm_gain,
           w_bo_gla, w_bo_diff, w_out, final_gain):
    f = lambda a: np.ascontiguousarray(np.asarray(a, dtype=np.float32))
    x_prompt, x_sample, c_prompt, c_sample = f(x_prompt), f(x_sample), f(c_prompt), f(c_sample)
    w_ada_, b_ada_, ng, w_in_ = f(w_ada)[0], f(b_ada)[0], f(norm_gain)[0], f(w_in)[0]
    wal, bal = f(w_alpha)[0], f(b_alpha)[0]
    C = _consts()
    walpha = np.zeros((33, 1024), np.float32)
    walpha[0:16, 0:512] = wal[0]
    walpha[16:32, 512:1024] = wal[1]
    walpha[32, 0:512] = bal[0]
    walpha[32, 512:1024] = bal[1]
    shared = dict(
        w_ada=w_ada_, b_adaT=np.ascontiguousarray(b_ada_.reshape(24, 128).T),
        b_gate=np.ascontiguousarray(b_ada_[2048:3072].reshape(1, D)),
        ngT=np.ascontiguousarray(ng.reshape(8, 128).T), w_in=w_in_, walpha=walpha,
        gla_gain=np.ascontiguousarray(np.tile(f(gla_norm_gain)[0], 4).reshape(1, 1024)),
        lqk=np.ascontiguousarray(np.concatenate([f(lambda_q)[0].reshape(-1), f(lambda_k)[0].reshape(-1)]).reshape(1, 256)),
        diff_gain=f(diff_norm_gain)[0].reshape(1, 128),
        w_bog=f(w_bo_gla)[0], w_bod=f(w_bo_diff)[0], w_out=f(w_out)[0],
        fgain=f(final_gain).reshape(1, D), ident=C["ident"], tri=C["tri"], msk=C["msk"], negs=C["negs"],
        rope_t=C["rope"],
    )
    in_maps = []
    for c in range(NC_):
        m = dict(shared)
        m["xs"] = x_sample[c]
        m["xo"] = np.ascontiguousarray(x_prompt[0, c * TO:(c + 1) * TO])
        cT = np.stack([c_sample[c].reshape(8, 128).T, c_prompt[0].reshape(8, 128).T], axis=-1)
        m["cT"] = np.ascontiguousarray(cT.astype(np.float32))
        m["rope_o"] = np.ascontiguousarray(C["rope"][c * TO:(c + 1) * TO])
        m["xr"] = np.ascontiguousarray(np.concatenate([x_prompt[0, :c * TO], x_prompt[0, (c + 1) * TO:]], axis=0))
        m["rope_r"] = np.ascontiguousarray(np.concatenate([C["rope"][:c * TO], C["rope"][(c + 1) * TO:]], axis=0))
        n = np.arange(128)
        mf = (n < 16 * c).astype(np.float32)
        mb = ((n >= 16 * c) & (n < 112)).astype(np.float32)
        m["segm"] = np.concatenate([mf, 1 - mf, mb, 1 - mb]).reshape(1, 512).astype(np.float32)
        in_maps.append(m)
    if "nc" not in _CACHE:
        _CACHE["nc"] = build()
    res = run_bass_kernel_spmd(_CACHE["nc"], in_maps, core_ids=list(range(NC_)))
    y_prompt = np.concatenate([res.results[c]["y_o"] for c in range(NC_)], axis=0)[None]
    y_sample = np.stack([res.results[c]["y_s"] for c in range(NC_)], axis=0)
    return (y_prompt.astype(np.float32), y_sample.astype(np.float32))
```
